# Optimizing a Trainium2 kernel written in Bass

```python
import jax, jax.numpy as jnp
from jax import lax
import numpy as np

D_MODEL = 2048
BATCH = 2
SEQ = 8192
DEPTH = 4

CHUNK = 64
HEAD_DIM = 128
MIX_WIDTH = D_MODEL
N_HEADS_FOX = MIX_WIDTH // (2 * HEAD_DIM)
N_HEADS_SB = MIX_WIDTH // (2 * HEAD_DIM)
WIDTH_FOX = N_HEADS_FOX * HEAD_DIM
WIDTH_SB = N_HEADS_SB * HEAD_DIM
IN_COLS = 3 * WIDTH_FOX + N_HEADS_FOX + 3 * WIDTH_SB
D_FF = ((8 * D_MODEL // 3 + 255) // 256) * 256
CONV_WIDTH = 3
Q_BLOCK = 128
EPS = 1e-6

kernel_name = "fox_stickbreak_hybrid_convffn"


def rms_norm(x, g):
    x32 = x.astype(jnp.float32)
    y = x32 * lax.rsqrt(jnp.mean(x32 * x32, axis=-1, keepdims=True) + EPS)
    return (y * g.astype(jnp.float32)).astype(x.dtype)


def to_heads(t, n_heads):
    b, s, _ = t.shape
    return t.reshape(b, s, n_heads, HEAD_DIM).transpose(0, 2, 1, 3)


def from_heads(t):
    b, h, s, d = t.shape
    return t.transpose(0, 2, 1, 3).reshape(b, s, h * d)


def to_blocks(t):
    b, h, s = t.shape[:3]
    nb = s // Q_BLOCK
    t = t.reshape((b, h, nb, Q_BLOCK) + t.shape[3:])
    return jnp.moveaxis(t, 2, 0)


def from_blocks(t):
    nb, b, h, qb, d = t.shape
    return jnp.moveaxis(t, 0, 2).reshape(b, h, nb * qb, d)


def forgetting_attention(q, k, v, c):
    s_len = k.shape[2]
    scale = HEAD_DIM ** -0.5
    k_pos = jnp.arange(s_len)
    k32 = k.astype(jnp.float32)
    c32 = c.astype(jnp.float32)

    def block(args):
        qi, ci, bi = args
        logits = jnp.einsum('bhqd,bhkd->bhqk', qi.astype(jnp.float32), k32) * scale
        logits = logits + ci[..., None] - c32[:, :, None, :]
        q_pos = bi * Q_BLOCK + jnp.arange(Q_BLOCK)
        mask = k_pos[None, :] <= q_pos[:, None]
        logits = jnp.where(mask, logits, -jnp.inf)
        p = jax.nn.softmax(logits, axis=-1)
        return jnp.einsum('bhqk,bhkd->bhqd', p.astype(v.dtype), v)

    nb = s_len // Q_BLOCK
    out = lax.map(block, (to_blocks(q), to_blocks(c32), jnp.arange(nb)))
    return from_blocks(out)


def stick_breaking_attention(q, k, v):
    s_len = k.shape[2]
    scale = HEAD_DIM ** -0.5
    k_pos = jnp.arange(s_len)
    k32 = k.astype(jnp.float32)

    def block(args):
        qi, bi = args
        z = jnp.einsum('bhqd,bhkd->bhqk', qi.astype(jnp.float32), k32) * scale
        q_pos = bi * Q_BLOCK + jnp.arange(Q_BLOCK)
        mask = k_pos[None, :] < q_pos[:, None]
        log_beta = jax.nn.log_sigmoid(z)
        log_one_minus = jnp.where(mask, log_beta - z, 0.0)
        key_axis = log_one_minus.ndim - 1
        suffix = lax.cumsum(log_one_minus, axis=key_axis, reverse=True) - log_one_minus
        weights = jnp.where(mask, jnp.exp(log_beta + suffix), 0.0)
        return jnp.einsum('bhqk,bhkd->bhqd', weights.astype(v.dtype), v)

    nb = s_len // Q_BLOCK
    out = lax.map(block, (to_blocks(q), jnp.arange(nb)))
    return from_blocks(out)


def hybrid_mixer(h, w_in_l, b_forget_l, q_norm_l, k_norm_l, out_norm_fox_l, out_norm_sb_l, w_out_l):
    proj = h @ w_in_l
    splits = [WIDTH_FOX, 2 * WIDTH_FOX, 3 * WIDTH_FOX, 3 * WIDTH_FOX + N_HEADS_FOX,
              3 * WIDTH_FOX + N_HEADS_FOX + WIDTH_SB, 3 * WIDTH_FOX + N_HEADS_FOX + 2 * WIDTH_SB]
    qa, ka, va, fa, qb, kb, vb = jnp.split(proj, splits, axis=-1)

    qa = rms_norm(to_heads(qa, N_HEADS_FOX), q_norm_l)
    ka = rms_norm(to_heads(ka, N_HEADS_FOX), k_norm_l)
    log_f = jax.nn.log_sigmoid((fa + b_forget_l).astype(jnp.float32))
    c = jnp.cumsum(log_f, axis=1).transpose(0, 2, 1)
    out_a = forgetting_attention(qa, ka, to_heads(va, N_HEADS_FOX), c)

    out_b = stick_breaking_attention(to_heads(qb, N_HEADS_SB), to_heads(kb, N_HEADS_SB),
                                     to_heads(vb, N_HEADS_SB))

    merged = jnp.concatenate([rms_norm(from_heads(out_a), out_norm_fox_l),
                              rms_norm(from_heads(out_b), out_norm_sb_l)], axis=-1)
    return merged @ w_out_l


def conv_ffn(h, w_up_l, conv_w_l, conv_b_l, w_down_l):
    u = h @ w_up_l
    c = u.shape[-1]
    u = lax.conv_general_dilated(
        u, conv_w_l[:, None, :].astype(u.dtype), window_strides=(1,),
        padding=[(CONV_WIDTH - 1, 0)], dimension_numbers=('NWC', 'WIO', 'NWC'),
        feature_group_count=c) + conv_b_l
    gate, val = jnp.split(u, 2, axis=-1)
    return (jax.nn.silu(gate) * val) @ w_down_l


def setup_inputs(seed: int = 0) -> dict:
    key = jax.random.key(seed)
    ks = jax.random.split(key, 14)
    nrm = jax.random.normal
    f32 = jnp.float32
    return {
        "x": nrm(ks[0], (BATCH, SEQ, D_MODEL), f32),
        "attn_norm": 1.0 + 0.02 * nrm(ks[1], (DEPTH, D_MODEL), f32),
        "w_in": nrm(ks[2], (DEPTH, D_MODEL, IN_COLS), f32) * D_MODEL ** -0.5,
        "b_forget": 3.0 + 0.5 * nrm(ks[3], (DEPTH, N_HEADS_FOX), f32),
        "q_norm": 1.0 + 0.02 * nrm(ks[4], (DEPTH, HEAD_DIM), f32),
        "k_norm": 1.0 + 0.02 * nrm(ks[5], (DEPTH, HEAD_DIM), f32),
        "out_norm_fox": 1.0 + 0.02 * nrm(ks[6], (DEPTH, WIDTH_FOX), f32),
        "out_norm_sb": 1.0 + 0.02 * nrm(ks[7], (DEPTH, WIDTH_SB), f32),
        "w_out": nrm(ks[8], (DEPTH, MIX_WIDTH, D_MODEL), f32) * MIX_WIDTH ** -0.5,
        "ffn_norm": 1.0 + 0.02 * nrm(ks[9], (DEPTH, D_MODEL), f32),
        "w_up": nrm(ks[10], (DEPTH, D_MODEL, 2 * D_FF), f32) * D_MODEL ** -0.5,
        "conv_w": nrm(ks[11], (DEPTH, CONV_WIDTH, 2 * D_FF), f32) * CONV_WIDTH ** -0.5,
        "conv_b": 0.02 * nrm(ks[12], (DEPTH, 2 * D_FF), f32),
        "w_down": nrm(ks[13], (DEPTH, D_FF, D_MODEL), f32) * D_FF ** -0.5,
    }


def reference(x, attn_norm, w_in, b_forget, q_norm, k_norm, out_norm_fox, out_norm_sb,
              w_out, ffn_norm, w_up, conv_w, conv_b, w_down):
    for layer in range(DEPTH):
        h = rms_norm(x, attn_norm[layer])
        x = x + hybrid_mixer(h, w_in[layer], b_forget[layer], q_norm[layer], k_norm[layer],
                             out_norm_fox[layer], out_norm_sb[layer], w_out[layer])
        h = rms_norm(x, ffn_norm[layer])
        x = x + conv_ffn(h, w_up[layer], conv_w[layer], conv_b[layer], w_down[layer])
    return x
```

```python
import numpy as np
from contextlib import ExitStack
import concourse.bass as bass
import concourse.mybir as mybir
from concourse.bass_utils import run_bass_kernel_spmd

F32 = mybir.dt.float32
BF16 = mybir.dt.bfloat16
AF = mybir.ActivationFunctionType
ALU = mybir.AluOpType

D = 2048
HD = 128
NH = 8
DFF = 5632
INC = 6152
KC = D // 128
FC = DFF // 128
EPS = 1e-6
SCALE = HD ** -0.5

ENGS = ['pe', 'act', 'dve', 'pool', 'sp']
EPOCH = 20000
DMA_POOL = 12
SAME_ENGINE_SYNC = True


class Tok:
    __slots__ = ('sem', 'val', 'eng')

    def __init__(self, sem, val, eng):
        self.sem = sem
        self.val = val
        self.eng = eng


class Sched:
    def __init__(self, nc, sems):
        self.nc = nc
        self.sem_iter = iter(sems)
        self.ops = {e: [] for e in ENGS}
        self.prog = {e: [] for e in ENGS}
        self.count = {e: 0 for e in ENGS}
        self.known = {e: {} for e in ENGS}
        self.res = {}
        self.dma_sems = {}
        self.dma_cnt = {}
        self.dma_n = {}
        self.cc_sem = None
        self.cc_cnt = 0

    def _new_sem(self):
        return next(self.sem_iter)

    def _need(self, eng, tok, waits):
        if tok is None:
            return
        if tok.eng == 'pe' and eng == 'pe':
            return
        if (not SAME_ENGINE_SYNC) and tok.eng == eng:
            return
        k = self.known[eng]
        sid = id(tok.sem)
        if k.get(sid, 0) >= tok.val:
            return
        k[sid] = tok.val
        for i, (s, v) in enumerate(waits):
            if s is tok.sem:
                waits[i] = (s, max(v, tok.val))
                return
        waits.append((tok.sem, tok.val))

    def _deps(self, eng, reads, writes, waits):
        for r in reads:
            st = self.res.get(r)
            if st is not None:
                self._need(eng, st[0], waits)
        for w in writes:
            st = self.res.get(w)
            if st is not None:
                self._need(eng, st[0], waits)
                for t in st[1].values():
                    self._need(eng, t, waits)

    def _commit(self, tok, reads, writes):
        for r in reads:
            st = self.res.get(r)
            if st is None:
                st = [None, {}]
                self.res[r] = st
            st[1][id(tok.sem)] = tok
        for w in writes:
            self.res[w] = [tok, {}]

    def op(self, eng, fn, reads=(), writes=()):
        waits = []
        self._deps(eng, reads, writes, waits)
        n = self.count[eng]
        ep, k = divmod(n, EPOCH)
        while len(self.prog[eng]) <= ep:
            self.prog[eng].append(self._new_sem())
        sem = self.prog[eng][ep]
        tok = Tok(sem, k + 1, eng)
        self.count[eng] = n + 1
        self.ops[eng].append((waits, fn, sem, 1))
        self._commit(tok, reads, writes)
        return tok

    def dma(self, q, fn, reads=(), writes=(), ndma=1):
        if q not in self.dma_sems:
            self.dma_sems[q] = [self._new_sem() for _ in range(DMA_POOL)]
            self.dma_cnt[q] = [0] * DMA_POOL
            self.dma_n[q] = 0
        waits = []
        self._deps(q, reads, writes, waits)
        slot = self.dma_n[q] % DMA_POOL
        self.dma_n[q] += 1
        sem = self.dma_sems[q][slot]
        prev = self.dma_cnt[q][slot]
        if prev > 0:
            self._need(q, Tok(sem, prev, 'dma'), waits)
        newv = prev + 16 * ndma
        self.dma_cnt[q][slot] = newv
        tok = Tok(sem, newv, 'dma')
        self.ops[q].append((waits, fn, sem, -ndma))
        self._commit(tok, reads, writes)
        return tok

    def collective(self, fn, reads, writes):
        if self.cc_sem is None:
            self.cc_sem = self._new_sem()
        waits = []
        self._deps('pool', reads, writes, waits)
        self.cc_cnt += 1
        tok = Tok(self.cc_sem, self.cc_cnt, 'cc')
        self.ops['pool'].append((waits, fn, self.cc_sem, 1))
        self._commit(tok, reads, writes)
        return tok

    def drain(self):
        toks = []
        for st in self.res.values():
            if st[0] is not None:
                toks.append(st[0])
            toks.extend(st[1].values())
        for e in ENGS:
            waits = []
            for t in toks:
                if t.eng == 'pe' and e == 'pe':
                    continue
                k = self.known[e]
                sid = id(t.sem)
                if k.get(sid, 0) >= t.val:
                    continue
                k[sid] = t.val
                for i, (s, v) in enumerate(waits):
                    if s is t.sem:
                        waits[i] = (s, max(v, t.val))
                        break
                else:
                    waits.append((t.sem, t.val))
            if waits:
                self.ops[e].append((waits, None, None, 0))
        self.res = {}

    def emit(self, block):
        names = {'pe': 'tensor', 'act': 'scalar', 'dve': 'vector', 'pool': 'gpsimd', 'sp': 'sync'}

        def make(e):
            lst = self.ops[e]

            def body(eng):
                for waits, fn, sem, inc in lst:
                    for s, v in waits:
                        eng.wait_ge(s, v)
                    if fn is None:
                        continue
                    if inc > 0:
                        fn(eng).then_inc(sem, 1)
                    else:
                        cnt = [0]

                        def incf(ins, sem=sem, cnt=cnt):
                            cnt[0] += 1
                            return ins.then_inc(sem, 16)
                        fn(eng, incf)
                        assert cnt[0] == -inc, (cnt[0], inc)
            return body

        for e in ENGS:
            if self.ops[e]:
                getattr(block, names[e])(make(e))
        self.ops = {e: [] for e in ENGS}


class Builder:
    def __init__(self, L, NB, dbg=(), ncores=8):
        self.ncores = ncores
        self.rg = [[0, 1, 2, 3], [4, 5, 6, 7]] if ncores == 8 else [[0, 1, 2, 3]]
        self.L = L
        self.NB = NB
        self.NT = NB * 128
        self.dbg = set(dbg)
        NT = self.NT
        self.KTF_OFF = 0
        self.VF_OFF = 8 * NT
        self.KTS_OFF = self.VF_OFF + 8 * NB * 129
        self.VS_OFF = self.KTS_OFF + 8 * NT
        self.CW = self.VS_OFF + 8 * NB * 128
        self.NQ = 4 * NB

    def mm(self, out, lhsT, rhs, start, stop, r, w, tr=False):
        if tr:
            self.S.op('pe', lambda e: e.matmul(out, lhsT=lhsT, rhs=rhs, start=start, stop=stop, is_transpose=True), r, w)
        else:
            self.S.op('pe', lambda e: e.matmul(out, lhsT=lhsT, rhs=rhs, start=start, stop=stop), r, w)

    def act(self, out, in_, func, r, w, bias=None, scale=None, accum=None):
        kw = {}
        if bias is not None:
            kw['bias'] = bias
        if scale is not None:
            kw['scale'] = scale
        if accum is not None:
            kw['accum_out'] = accum
        self.S.op('act', lambda e: e.activation(out=out, in_=in_, func=func, **kw), r, w)

    def ts(self, eng, out, in0, s1, s2, op0, op1, r, w):
        if op1 is None:
            self.S.op(eng, lambda e: e.tensor_scalar(out=out, in0=in0, scalar1=s1, scalar2=None, op0=op0), r, w)
        else:
            self.S.op(eng, lambda e: e.tensor_scalar(out=out, in0=in0, scalar1=s1, scalar2=s2, op0=op0, op1=op1), r, w)

    def tt(self, eng, out, in0, in1, op, r, w):
        self.S.op(eng, lambda e: e.tensor_tensor(out=out, in0=in0, in1=in1, op=op), r, w)

    def stt(self, out, in0, scalar, in1, op0, op1, r, w):
        self.S.op('dve', lambda e: e.scalar_tensor_tensor(out=out, in0=in0, scalar=scalar, in1=in1, op0=op0, op1=op1), r, w)

    def cp(self, eng, out, in_, r, w):
        if eng == 'act':
            self.S.op('act', lambda e: e.copy(out=out, in_=in_), r, w)
        else:
            self.S.op(eng, lambda e: e.tensor_copy(out=out, in_=in_), r, w)

    def memset(self, eng, ap, val, w):
        self.S.op(eng, lambda e: e.memset(ap, val), (), w)

    def dma(self, out, in_, r, w, q='sp'):
        self.S.dma(q, lambda e, inc: inc(e.dma_start(out=out, in_=in_)), r, w)

    def rstd_from_ssq(self, ssq, tmp, n, r, w):
        self.ts('dve', tmp, ssq, 1.0 / n, EPS, ALU.mult, ALU.add, r, w)
        self.act(tmp, tmp, AF.Sqrt, w, w)
        self.S.op('dve', lambda e: e.reciprocal(out=tmp, in_=tmp), w, w)

    def build(self):
        L, NB, NT = self.L, self.NB, self.NT
        nc = bass.Bass("TRN2", target_bir_lowering=False)
        self.nc = nc
        dt_in = lambda n, s, dt=F32: nc.dram_tensor(n, s, dt, kind="ExternalInput").ap()
        I = {}
        I['x'] = dt_in("x", [NT, D])
        I['attn_norm'] = dt_in("attn_norm", [L, D])
        I['w_in'] = dt_in("w_in", [L, D, INC])
        I['b_forget'] = dt_in("b_forget", [L, NH])
        I['q_norm'] = dt_in("q_norm", [L, HD])
        I['k_norm'] = dt_in("k_norm", [L, HD])
        I['out_norm'] = dt_in("out_norm", [L, D])
        I['w_out'] = dt_in("w_out", [L, D, D])
        I['ffn_norm'] = dt_in("ffn_norm", [L, D])
        I['w_up'] = dt_in("w_up", [L, D, 2 * DFF])
        I['conv_wb'] = dt_in("conv_wb", [L, 4, 2 * DFF])
        I['w_down'] = dt_in("w_down", [L, DFF, D])
        I['cst'] = dt_in("cst", [128, 1408])
        I['cst2'] = dt_in("cst2", [128, 256])
        self.I = I
        y = nc.dram_tensor("y", [NT, D], F32, kind="ExternalOutput").ap()
        self.y = y
        dti = lambda n, s, dt: nc.dram_tensor(n, s, dt).ap()
        self.xa_d = dti("xa_d", [NT, D], F32)
        self.xres_d = dti("xres_d", [NT, D], F32)
        self.qT_d = dti("qT_d", [16, 128, NT], BF16)
        self.h2T_d = dti("h2T_d", [128, KC, NB, 130], BF16)
        self.g_in = {}
        self.g_out = {}
        for p in range(2):
            for hp in range(4):
                for nm, w in (('kf', 2 * NT), ('ks', 2 * NT), ('vs', 2 * NB * 128)):
                    self.g_in[(nm, p, hp)] = dti(f"gi_{nm}{p}_{hp}", [128, w], BF16)
                    self.g_out[(nm, p, hp)] = dti(f"go_{nm}{p}_{hp}", [512, w], BF16)
            for h in range(8):
                self.g_in[('vf', p, h)] = dti(f"gi_vf{p}_{h}", [128, NB * 129], BF16)
                self.g_out[('vf', p, h)] = dti(f"go_vf{p}_{h}", [512, NB * 129], BF16)
        self.gin2 = [dti(f"gin2_{p}", [129, NB * 8], F32) for p in range(2)]
        self.gout2 = [dti(f"gout2_{p}", [4 * 129, NB * 8], F32) for p in range(2)]
        self.gin3 = [dti(f"gin3_{p}", [NB * 2, D], BF16) for p in range(2)]
        self.gout3 = [dti(f"gout3_{p}", [4 * NB * 2, D], BF16) for p in range(2)]
        self.dbg_out = {}
        if 'merged' in self.dbg:
            self.dbg_out['merged'] = nc.dram_tensor("dbg_merged", [128, NB, D], BF16, kind="ExternalOutput").ap()
        if 'xa' in self.dbg:
            self.dbg_out['xa'] = nc.dram_tensor("dbg_xa", [NT, D], F32, kind="ExternalOutput").ap()

        with ExitStack() as es:
            sems = [es.enter_context(nc.semaphore(f"s{i}")) for i in range(64)]
            self.S = Sched(nc, sems)
            self.ps = [es.enter_context(nc.psum_tensor(f"ps{k}", [128, 512], F32)) for k in range(8)]
            self.psb = [p[:].bitcast(BF16) for p in self.ps]
            sb = lambda n, s, dt: es.enter_context(nc.sbuf_tensor("c_" + n, s, dt))
            self.cst_f = sb("cst_f", [128, 1408], F32)
            self.identb = sb("identb", [128, 128], BF16)
            self.Jb = sb("Jb", [128, 128], BF16)
            self.onesb = sb("onesb", [128, 128], BF16)
            self.mfox = sb("mfox", [128, 512], BF16)
            self.cst2 = sb("cst2", [128, 256], F32)
            self.selb = sb("selb", [128, 2 * NB], BF16)
            self.zeros = sb("zeros", [128, 512], F32)
            self.ones_f = sb("ones_f", [128, 128], F32)
            self.tri = self.cst_f[:, 256:384]
            self.nmsb = self.cst_f[:, 896:1408]
            self.ident_f = self.cst_f[:, 0:128]
            self.phase_consts()
            for l in range(L):
                self.layer(l)
        return nc

    def run_block(self, fn):
        import os
        self._nph = getattr(self, '_nph', 0) + 1
        if self._nph > int(os.environ.get('MAXPH', '1000')):
            return
        fn()
        self.flush()

    def flush(self):
        if not any(self.S.ops[e] for e in ENGS):
            return
        self.S.drain()
        with self.nc.Block() as block:
            self.S.emit(block)

    def phase_consts(self):
        def f():
            I = self.I
            self.dma(self.cst_f[:, 0:384], I['cst'][:, 0:384], (), ['cst_f'])
            self.dma(self.cst_f[:, 384:1408], I['cst'][:, 384:1408], (), ['cst_f2'])
            self.dma(self.cst2[:], I['cst2'], (), ['cst2'])
            self.cp('dve', self.identb[:], self.cst_f[:, 0:128], ['cst_f'], ['identb'])
            self.cp('dve', self.Jb[:], self.cst_f[:, 128:256], ['cst_f'], ['Jb'])
            self.cp('dve', self.mfox[:], self.cst_f[:, 384:896], ['cst_f2'], ['mfox'])
            self.cp('dve', self.selb[:], self.cst2[:, 128:128 + 2 * self.NB], ['cst2'], ['selb'])
            self.memset('pool', self.onesb[:], 1.0, ['onesb'])
            self.memset('pool', self.zeros[:], 0.0, ['zeros'])
            self.memset('pool', self.ones_f[:], 1.0, ['ones_f'])
        self.run_block(f)

    def layer(self, l):
        nc = self.nc
        NB, NT = self.NB, self.NT
        par = l % 2
        xsrc = self.I['x'] if l == 0 else self.xres_d
        xdst = self.y if l == self.L - 1 else self.xres_d
        with ExitStack() as es1:
            hT = es1.enter_context(nc.sbuf_tensor(f"hT{l}", [128, KC, NT], BF16))
            self.run_block(lambda: self.phase_norm_T(l, xsrc, self.I['attn_norm'][l], hT, 'A1', halo=None))
            self.run_block(lambda: self.phase_A2(l, hT))
        def g():
            S = self.S
            gi2, go2 = self.gin2[par], self.gout2[par]
            rg = self.rg
            for key in self.g_in:
                if key[1] != par:
                    continue
                S.collective((lambda a, b: (lambda e: e.collective_compute("AllGather", ALU.bypass, replica_groups=rg, ins=[a], outs=[b])))(
                    self.g_in[key], self.g_out[key]), [], [('gout', key)])
            S.collective(lambda e: e.collective_compute("AllGather", ALU.bypass, replica_groups=rg, ins=[gi2], outs=[go2]), [], ['gout2'])
        self.run_block(g)
        with ExitStack() as es2:
            merged = es2.enter_context(nc.sbuf_tensor(f"merged{l}", [128, NB, D], BF16))
            self.run_block(lambda: self.phase_B(l, merged))
            mT = es2.enter_context(nc.sbuf_tensor(f"mT{l}", [128, KC, NT], BF16))
            self.run_block(lambda: self.phase_norm_T(l, None, self.I['out_norm'][l], mT, 'C1', halo=None, merged=merged))
            self.run_block(lambda: self.phase_C2(l, mT, xsrc))
        with ExitStack() as es3:
            halo_s = es3.enter_context(nc.sbuf_tensor(f"halo{l}", [128, KC, NB, 2], BF16))
            self.run_block(lambda: self.phase_norm_T(l, self.xa_d, self.I['ffn_norm'][l], None, 'C3', halo=par))
            def g3():
                rg = self.rg
                gi, go = self.gin3[par], self.gout3[par]
                self.S.collective(lambda e: e.collective_compute("AllGather", ALU.bypass, replica_groups=rg, ins=[gi], outs=[go]), [], ['gout3'])
            self.run_block(g3)
            self.run_block(lambda: self.phase_halo(l, halo_s))
            self.run_block(lambda: self.phase_FFN(l, halo_s, xdst))

    def phase_norm_T(self, l, xsrc, gvec, dstT, tag, halo, merged=None):
        nc = self.nc
        NB = self.NB
        with ExitStack() as es:
            sb = lambda n, s, dt: es.enter_context(nc.sbuf_tensor(f"{tag}{l}_{n}", s, dt))
            gB = sb("gB", [128, D], F32)
            hb = [sb(f"hb{k}", [128, D], BF16) for k in range(2)]
            junk = sb("junk", [128, D], BF16)
            ssq = sb("ssq", [128, 2 * NB], F32)
            rstd = sb("rstd", [128, 2 * NB], F32)
            xb = None
            if merged is None:
                xb = [sb(f"xb{k}", [128, D], F32) for k in range(2)]
            hts = None
            if tag == 'C3':
                hts = [sb(f"hts{k}", [128, KC, 4, 130], BF16) for k in range(2)]
            self.dma(gB[:], gvec.partition_broadcast(128), (), ['gB'])
            for lb in range(NB):
                b2 = lb % 2
                if merged is None:
                    self.dma(xb[b2][:], xsrc[lb * 128:(lb + 1) * 128, :], [], [('xb', b2)])
                    self.act(junk[:], xb[b2][:], AF.Square, [('xb', b2)], ['junk', ('ssq', lb)], accum=ssq[:, lb:lb + 1])
                    self.rstd_from_ssq(ssq[:, lb:lb + 1], rstd[:, lb:lb + 1], D, [('ssq', lb)], [('rstd', lb)])
                    self.stt(hb[b2][:], xb[b2][:], rstd[:, lb:lb + 1], gB[:], ALU.mult, ALU.mult,
                             [('xb', b2), ('rstd', lb), 'gB'], [('hb', b2)])
                else:
                    for g in range(2):
                        self.act(junk[:, 0:1024], merged[:, lb, g * 1024:(g + 1) * 1024], AF.Square, ['merged'],
                                 ['junk', ('ssq', lb)], accum=ssq[:, 2 * lb + g:2 * lb + g + 1])
                    self.rstd_from_ssq(ssq[:, 2 * lb:2 * lb + 2], rstd[:, 2 * lb:2 * lb + 2], 1024, [('ssq', lb)], [('rstd', lb)])
                    for g in range(2):
                        self.stt(hb[b2][:, g * 1024:(g + 1) * 1024], merged[:, lb, g * 1024:(g + 1) * 1024],
                                 rstd[:, 2 * lb + g:2 * lb + g + 1], gB[:, g * 1024:(g + 1) * 1024], ALU.mult, ALU.mult,
                                 ['merged', ('rstd', lb), 'gB'], [('hb', b2)])
                if tag == 'C3':
                    self.dma(self.gin3[halo][2 * lb:2 * lb + 2, :], hb[b2][126:128, :], [('hb', b2)], [('gin3', lb)])
                for half in range(2):
                    bank = 2 * (lb % 2) + half
                    for k8 in range(8):
                        kc = half * 8 + k8
                        self.mm(self.psb[bank][:, k8 * 128:(k8 + 1) * 128], hb[b2][:, kc * 128:(kc + 1) * 128], self.identb[:],
                                True, True, [('hb', b2), 'identb'], [('ps', bank)], tr=True)
                    src = self.psb[bank][:, 0:1024].rearrange("p (a b) -> p a b", a=8)
                    eng = 'act' if half == 0 else 'dve'
                    if tag == 'C3':
                        s4 = (lb // 4) % 2
                        self.cp(eng, hts[s4][:, half * 8:half * 8 + 8, lb % 4, 2:130], src, [('ps', bank)], [('hts', s4, half, lb % 4)])
                    else:
                        self.cp(eng, dstT[:, half * 8:half * 8 + 8, lb * 128:(lb + 1) * 128], src, [('ps', bank)], [('dstT', lb, half)])
                if tag == 'C3' and (lb % 4 == 3 or lb == NB - 1):
                    s4 = (lb // 4) % 2
                    nb4 = lb % 4 + 1
                    lb0 = lb - (nb4 - 1)
                    self.dma(self.h2T_d[:, :, lb0:lb0 + nb4, :], hts[s4][:, :, 0:nb4, :],
                             [('hts', s4, h, q) for h in range(2) for q in range(nb4)], [('h2T_d', lb0)])
            self.flush()

    def wload(self, wsrc, rows_kc, c0, ncols, wst, wb, slot, rname, row0=0):
        src = wsrc[row0:row0 + rows_kc * 128, c0:c0 + ncols].rearrange("(kc p) c -> p kc c", p=128)
        self.dma(wst[:, 0:rows_kc, 0:ncols], src, [], [(rname + 'st', slot)])
        self.cp('pool', wb[:, 0:rows_kc, 0:ncols], wst[:, 0:rows_kc, 0:ncols], [(rname + 'st', slot)], [(rname, slot)])

    def phase_A2(self, l, hT):
        nc = self.nc
        NB, NT = self.NB, self.NT
        I = self.I
        par = l % 2
        gin2 = self.gin2[par]
        GI = self.g_in
        W = I['w_in'][l]
        NTT = (NT + 511) // 512
        with ExitStack() as es:
            sb = lambda n, s, dt: es.enter_context(nc.sbuf_tensor(f"A2{l}_{n}", s, dt))
            wst = [sb(f"wst{k}", [128, KC, 256], F32) for k in range(2)]
            wb = [sb(f"wb{k}", [128, KC, 256], BF16) for k in range(2)]
            ost = [sb(f"ost{k}", [128, NT], BF16) for k in range(2)]
            vst = [sb(f"vst{k}", [128, 2, NB, 129], BF16) for k in range(2)]
            sqb = [sb(f"sqb{k}", [128, 512], BF16) for k in range(2)]
            rs = [sb(f"rs{k}", [128, 512], F32) for k in range(2)]
            tm = [sb(f"tm{k}", [128, 256], BF16) for k in range(2)]
            gq = sb("gq", [128, 2], F32)
            wfst = sb("wfst", [128, KC, 8], F32)
            wfb = sb("wfb", [128, KC, 8], BF16)
            bfB = sb("bfB", [128, 8], F32)
            lg = sb("lg", [128, NB, 8], F32)
            cwst = sb("cwst", [128, NB * 8], F32)
            for k in range(2):
                self.memset('pool', vst[k][:, :, :, 128:129], 1.0, [('vst', k)])
            self.dma(gq[:, 0:1], I['q_norm'][l].rearrange("(p o) -> p o", o=1), [], ['gq0'])
            self.dma(gq[:, 1:2], I['k_norm'][l].rearrange("(p o) -> p o", o=1), [], ['gq1'])
            self.dma(wfst[:], W[:, 3072:3080].rearrange("(kc p) c -> p kc c", p=128), [], ['wfst'])
            self.dma(bfB[:], I['b_forget'][l].partition_broadcast(128), [], ['bfB'])
            self.cp('pool', wfb[:], wfst[:], ['wfst'], ['wfb'])

            groups = []
            for gI in range(4):
                groups.append((gI * 256, 'fq', 2 * gI))
            for gI in range(4):
                groups.append((1024 + gI * 256, 'fk', 2 * gI))
            for gI in range(4):
                groups.append((2048 + gI * 256, 'fv', 2 * gI))
            for gI in range(4):
                groups.append((3080 + gI * 256, 'sq', 2 * gI))
            for gI in range(4):
                groups.append((4104 + gI * 256, 'sk', 2 * gI))
            for gI in range(4):
                groups.append((5128 + gI * 256, 'sv', 2 * gI))
            self.wload(W, KC, groups[0][0], 256, wst[0], wb[0], 0, 'w')
            ostn = 0
            bankrr = 0
            for n, (c0, kind, h0) in enumerate(groups):
                slot = n % 2
                if n + 1 < len(groups):
                    self.wload(W, KC, groups[n + 1][0], 256, wst[(n + 1) % 2], wb[(n + 1) % 2], (n + 1) % 2, 'w')
                if kind in ('fq', 'fk', 'sq'):
                    for hh in range(2):
                        head = h0 + hh
                        o = ost[ostn % 2]
                        orn = ('ost', ostn % 2)
                        ostn += 1
                        for tt in range(NTT):
                            t0 = tt * 512
                            tn = min(512, NT - t0)
                            bank = bankrr % 4
                            bankrr += 1
                            for kc in range(KC):
                                self.mm(self.ps[bank][:, 0:tn], wb[slot][:, kc, hh * 128:(hh + 1) * 128], hT[:, kc, t0:t0 + tn],
                                        kc == 0, kc == KC - 1, [('w', slot), 'hT'], [('ps', bank)])
                            if kind == 'sq':
                                self.cp('act', o[:, t0:t0 + tn], self.ps[bank][:, 0:tn], [('ps', bank)], [orn])
                            else:
                                k2 = tt % 2
                                gcol = gq[:, 0:1] if kind == 'fq' else gq[:, 1:2]
                                self.act(sqb[k2][:, 0:tn], self.ps[bank][:, 0:tn], AF.Square, [('ps', bank)], [('sqb', k2)])
                                sb_ = 4 + k2
                                self.mm(self.ps[sb_][:, 0:tn], self.onesb[:], sqb[k2][:, 0:tn], True, True,
                                        [('sqb', k2), 'onesb'], [('ps', sb_)])
                                self.ts('dve', rs[k2][:, 0:tn], self.ps[sb_][:, 0:tn], 1.0 / HD, EPS, ALU.mult, ALU.add,
                                        [('ps', sb_)], [('rs', k2)])
                                self.act(rs[k2][:, 0:tn], rs[k2][:, 0:tn], AF.Sqrt, [('rs', k2)], [('rs', k2)])
                                self.S.op('dve', (lambda a: (lambda e: e.reciprocal(out=a, in_=a)))(rs[k2][:, 0:tn]), [('rs', k2)], [('rs', k2)])
                                self.stt(o[:, t0:t0 + tn], self.ps[bank][:, 0:tn], gcol, rs[k2][:, 0:tn], ALU.mult, ALU.mult,
                                         [('ps', bank), ('rs', k2), 'gq0', 'gq1'], [orn])
                        if kind == 'fq':
                            self.dma(self.qT_d[head], o[:], [orn], [('qT_d', head)])
                        elif kind == 'sq':
                            self.dma(self.qT_d[8 + head], o[:], [orn], [('qT_d', 8 + head)])
                        else:
                            self.dma(GI[('kf', par, head // 2)][:, (head % 2) * NT:(head % 2 + 1) * NT], o[:], [orn], [('gin', 'kf', head)])
                elif kind == 'fv':
                    v = vst[(n // 1) % 2]
                    vrn = ('vst', n % 2)
                    for lb in range(NB):
                        bank = bankrr % 4
                        bankrr += 1
                        for kc in range(KC):
                            self.mm(self.ps[bank][:, 0:256], hT[:, kc, lb * 128:(lb + 1) * 128], wb[slot][:, kc, :],
                                    kc == 0, kc == KC - 1, [('w', slot), 'hT'], [('ps', bank)])
                        self.cp('act', v[:, :, lb, 0:128], self.ps[bank][:, 0:256].rearrange("p (a b) -> p a b", a=2),
                                [('ps', bank)], [vrn])
                    for hh in range(2):
                        head = h0 + hh
                        self.dma(GI[('vf', par, head)][:, :], v[:, hh].rearrange("p a b -> p (a b)"), [vrn], [('gin', 'vf', head)])
                elif kind == 'sk':
                    o2 = [ost[ostn % 2], ost[(ostn + 1) % 2]]
                    orn2 = [('ost', ostn % 2), ('ost', (ostn + 1) % 2)]
                    ostn += 2
                    for lb in range(NB):
                        bank = bankrr % 4
                        bankrr += 1
                        for kc in range(KC):
                            self.mm(self.ps[bank][:, 0:256], hT[:, kc, lb * 128:(lb + 1) * 128], wb[slot][:, kc, :],
                                    kc == 0, kc == KC - 1, [('w', slot), 'hT'], [('ps', bank)])
                        k2 = lb % 2
                        self.cp('act', tm[k2][:], self.ps[bank][:, 0:256], [('ps', bank)], [('tm', k2)])
                        jb = 4 + k2
                        for hh in range(2):
                            self.mm(self.ps[jb][:, hh * 128:(hh + 1) * 128], tm[k2][:, hh * 128:(hh + 1) * 128], self.Jb[:], True, True,
                                    [('tm', k2), 'Jb'], [('ps', jb)])
                        for hh in range(2):
                            self.cp('dve', o2[hh][:, lb * 128:(lb + 1) * 128], self.ps[jb][:, hh * 128:(hh + 1) * 128], [('ps', jb)], [orn2[hh]])
                    for hh in range(2):
                        head = h0 + hh
                        self.dma(GI[('ks', par, head // 2)][:, (head % 2) * NT:(head % 2 + 1) * NT], o2[hh][:], [orn2[hh]], [('gin', 'ks', head)])
                elif kind == 'sv':
                    v = vst[n % 2]
                    vrn = ('vst', n % 2)
                    for lb in range(NB):
                        bank = bankrr % 4
                        bankrr += 1
                        for kc in range(KC):
                            self.mm(self.ps[bank][:, 0:256], hT[:, kc, lb * 128:(lb + 1) * 128], wb[slot][:, kc, :],
                                    kc == 0, kc == KC - 1, [('w', slot), 'hT'], [('ps', bank)])
                        k2 = lb % 2
                        self.cp('act', tm[k2][:], self.ps[bank][:, 0:256], [('ps', bank)], [('tm', k2)])
                        jb = 4 + k2
                        self.mm(self.ps[jb][:, 0:256], self.Jb[:], tm[k2][:], True, True, [('tm', k2), 'Jb'], [('ps', jb)])
                        self.cp('dve', v[:, :, lb, 0:128], self.ps[jb][:, 0:256].rearrange("p (a b) -> p a b", a=2), [('ps', jb)], [vrn])
                    for hh in range(2):
                        head = h0 + hh
                        c = (head % 2) * NB * 128
                        self.dma(GI[('vs', par, head // 2)][:, c:c + NB * 128].rearrange("p (a b) -> p a b", a=NB), v[:, hh, :, 0:128], [vrn], [('gin', 'vs', head)])
            for lb in range(NB):
                bank = 6 + lb % 2
                for kc in range(KC):
                    self.mm(self.ps[bank][:, 0:8], hT[:, kc, lb * 128:(lb + 1) * 128], wfb[:, kc, :], kc == 0, kc == KC - 1,
                            ['wfb', 'hT'], [('ps', bank)])
                self.tt('dve', lg[:, lb, :], self.ps[bank][:, 0:8], bfB[:], ALU.add, [('ps', bank), 'bfB'], [('lg', lb)])
                self.act(lg[:, lb, :], lg[:, lb, :], AF.Exp, [('lg', lb)], [('lg', lb)], scale=-1.0)
                self.act(lg[:, lb, :], lg[:, lb, :], AF.Ln, [('lg', lb)], [('lg', lb)], bias=1.0)
                self.mm(self.ps[bank][:, 16:24], self.tri, lg[:, lb, :], True, True, [('lg', lb), 'cst_f'], [('ps', bank)])
                self.cp('dve', cwst[:, lb * 8:(lb + 1) * 8], self.ps[bank][:, 16:24], [('ps', bank)], ['cwst'])
            self.dma(gin2[0:128, :], cwst[:], ['cwst'], ['gin2a'])
            self.dma(gin2[128:129, :], cwst[127:128, :], ['cwst'], ['gin2b'])
            self.flush()

    def phase_B(self, l, merged):
        nc = self.nc
        NB, NT, NQ = self.NB, self.NT, self.NQ
        par = l % 2
        gout2 = self.gout2[par]
        GO = self.g_out
        with ExitStack() as es:
            sb = lambda n, s, dt: es.enter_context(nc.sbuf_tensor(f"B{l}_{n}", s, dt))
            KT = [sb(f"KT{k}", [128, 4, NT], BF16) for k in range(2)]
            VV = [sb(f"VV{k}", [128, 4, NB * 129], BF16) for k in range(2)]
            QT = [sb(f"QT{k}", [128, NT], BF16) for k in range(2)]
            cwS = sb("cwS", [128, 4, NB * 8], F32)
            bsT = sb("bsT", [64, 8], F32)
            bsbc = sb("bsbc", [64, 128], F32)
            Dt = sb("Dt", [128, NQ], F32)
            own = sb("own", [128, NB], F32)
            bias4 = [sb(f"bias4_{k}", [128, 4, NQ], F32) for k in range(2)]
            NPT = 6
            pT = [sb(f"pT{k}", [128, 128], BF16) for k in range(NPT)]
            rden = sb("rden", [128, 8], F32)
            NA = 3
            a_ = [sb(f"a{k}", [128, 512], F32) for k in range(NA)]
            buf = [sb(f"buf{k}", [128, 513], F32) for k in range(NA)]
            Ab = [sb(f"Ab{k}", [128, 512], BF16) for k in range(2)]
            AT = [sb(f"AT{k}", [128, 512], BF16) for k in range(2)]
            Lall = self.cst2[0:NQ, 0:NQ] if NQ <= 64 else None
            Lown = self.cst2[0:NQ, 64:64 + NB]

            self.dma(cwS[:], gout2.rearrange("(r q) c -> q r c", q=129)[0:128], [], ['cwS'])
            for r in range(4):
                self.dma(bsT[r * NB:(r + 1) * NB, :], gout2[r * 129 + 128, :].rearrange("(a b) -> a b", b=8), [], [('bsT', r)])
            bsT_r = [('bsT', r) for r in range(4)]

            def load_head(kind, h, k):
                if kind == 'f':
                    c = (h % 2) * NT
                    self.dma(KT[k][:], GO[('kf', par, h // 2)].rearrange("(r p) c -> p r c", p=128)[:, :, c:c + NT], [], [('KT', k, r) for r in range(4)])
                    self.dma(VV[k][:], GO[('vf', par, h)].rearrange("(r p) c -> p r c", p=128), [], [('VV', k, r) for r in range(4)])
                    self.dma(QT[k][:], self.qT_d[h], [], [('QT', k)])
                else:
                    c = (h % 2) * NT
                    cv = (h % 2) * NB * 128
                    gk_ = GO[('ks', par, h // 2)]
                    gv_ = GO[('vs', par, h // 2)]
                    for r in range(4):
                        self.dma(KT[k][:, 3 - r, :], gk_[r * 128:(r + 1) * 128, c:c + NT], [], [('KT', k, r)])
                        self.dma(VV[k][:, 3 - r, 0:NB * 128], gv_[r * 128:(r + 1) * 128, cv:cv + NB * 128], [], [('VV', k, r)])
                    self.dma(QT[k][:], self.qT_d[8 + h], [], [('QT', k)])

            heads = [('f', h) for h in range(NH)] + [('s', h) for h in range(NH)]
            load_head(heads[0][0], heads[0][1], 0)
            ptn = 0
            stn = 0
            for hn, (kind, h) in enumerate(heads):
                k = hn % 2
                if hn + 1 < len(heads):
                    load_head(heads[hn + 1][0], heads[hn + 1][1], (hn + 1) % 2)
                KTk, VVk, QTk = KT[k], VV[k], QT[k]
                KTr = [('KT', k, r) for r in range(4)]
                VVr = [('VV', k, r) for r in range(4)]
                if kind == 'f':
                    self.ts('dve', bsbc[0:NQ, :], self.ones_f[0:NQ, :], bsT[0:NQ, h:h + 1], None, ALU.mult, None,
                            bsT_r + ['ones_f'], ['bsbc'])
                    self.mm(self.ps[6][:, 0:NQ], bsbc[0:NQ, :], Lall, True, True, ['bsbc', 'cst2'], [('ps', 6)])
                    self.mm(self.ps[7][:, 0:NB], bsbc[0:NQ, :], Lown, True, True, ['bsbc', 'cst2'], [('ps', 7)])
                    self.tt('dve', Dt[:].rearrange("p (r a) -> p r a", r=4),
                            cwS[:].rearrange("p r (a c) -> p r a c", c=8)[:, :, :, h], self.ps[6][:, 0:NQ].rearrange("p (r a) -> p r a", r=4),
                            ALU.add, ['cwS', ('ps', 6)], ['Dt'])
                    self.cp('dve', own[:], self.ps[7][:, 0:NB], [('ps', 7)], ['own'])
                    for i0 in range(0, NB, 4):
                        i1 = min(i0 + 4, NB)
                        gk = (i0 // 4) % 2
                        for i in range(i0, i1):
                            self.ts('dve', bias4[gk][:, i - i0, :], Dt[:], own[:, i:i + 1], None, ALU.subtract, None,
                                    ['Dt', 'own'], [('bias4', gk, i - i0)])
                            dg = bias4[gk][:, i - i0, :].rearrange("p (r a) -> p r a", r=4)[:, :, i]
                            self.tt('dve', dg, dg, self.cst2[:, 200:204], ALU.add, [('bias4', gk, i - i0), 'cst2'], [('bias4', gk, i - i0)])
                        steps = [(ip, r) for ip in range(i1) for r in range(4)]

                        def qk(n):
                            ip, r = steps[n]
                            ist = max(ip, i0)
                            ncol = (i1 - ist) * 128
                            bank = n % 2
                            self.mm(self.ps[bank][:, 0:ncol], KTk[:, r, ip * 128:(ip + 1) * 128], QTk[:, ist * 128:i1 * 128], True, True,
                                    KTr + [('QT', k)], [('ps', bank)])
                        qk(0)
                        for n, (ip, r) in enumerate(steps):
                            if n + 1 < len(steps):
                                qk(n + 1)
                            ist = max(ip, i0)
                            bank = n % 2
                            q = r * NB + ip
                            for i in range(ist, i1):
                                pk = ptn % NPT
                                ptn += 1
                                self.act(pT[pk][:], self.ps[bank][:, (i - ist) * 128:(i - ist + 1) * 128], AF.Exp,
                                         [('ps', bank), ('bias4', gk, i - i0)], [('pT', pk)],
                                         bias=bias4[gk][:, i - i0, q:q + 1], scale=SCALE)
                                if ip == i:
                                    self.tt('pool', pT[pk][:], pT[pk][:], self.mfox[:, r * 128:(r + 1) * 128], ALU.mult,
                                            [('pT', pk), 'mfox'], [('pT', pk)])
                                ob = 2 + (i - i0)
                                self.mm(self.ps[ob][:, 0:129], pT[pk][:], VVk[:, r, ip * 129:ip * 129 + 129],
                                        (ip == 0 and r == 0), (ip == i and r == 3), [('pT', pk)] + VVr, [('ps', ob)])
                                if ip == i and r == 3:
                                    rc = rden[:, (i - i0):(i - i0) + 1]
                                    self.S.op('dve', (lambda a, b: (lambda e: e.reciprocal(out=a, in_=b)))(rc, self.ps[ob][:, 128:129]),
                                              [('ps', ob)], [('rden', i - i0)])
                                    self.ts('dve', merged[:, i, h * 128:(h + 1) * 128], self.ps[ob][:, 0:128], rc, None, ALU.mult, None,
                                            [('ps', ob), ('rden', i - i0)], [('merged', i, h)])
                else:
                    steps = [(i, n) for i in range(NB) for n in range(i + 1)]
                    NS = len(steps)

                    def st0(m):
                        i, n = steps[m]
                        ip = i - n
                        bank = m % 2
                        ak = m % NA
                        self.mm(self.ps[bank][:, 0:512].rearrange("p (a b) -> p a b", a=4), QTk[:, i * 128:(i + 1) * 128],
                                KTk[:, :, ip * 128:(ip + 1) * 128], True, True, KTr + [('QT', k)], [('ps', bank)])
                        self.act(a_[ak][:], self.ps[bank][:, 0:512], AF.Sigmoid, [('ps', bank)], [('a', ak)], scale=-SCALE)
                        if n == 0:
                            self.tt('dve', a_[ak][:], a_[ak][:], self.nmsb, ALU.max, [('a', ak), 'cst_f2'], [('a', ak)])

                    def st1(m):
                        i, n = steps[m]
                        ak = m % NA
                        if n == 0:
                            self.memset('pool', buf[ak][:, 0:1], 1.0, [('bufc', ak)])
                        else:
                            pk_ = (m - 1) % NA
                            self.cp('pool', buf[ak][:, 0:1], buf[pk_][:, 512:513], [('buf', pk_)], [('bufc', ak)])
                        self.S.op('dve', (lambda o, d0, ini: (lambda e: e.tensor_tensor_scan(out=o, data0=d0, data1=self.zeros[:], initial=ini,
                                                                                           op0=ALU.mult, op1=ALU.add)))(
                            buf[ak][:, 1:513], a_[ak][:], buf[ak][:, 0:1]), [('a', ak), ('bufc', ak), 'zeros'], [('buf', ak)])
                        self.tt('pool', Ab[m % 2][:], buf[ak][:, 0:512], buf[ak][:, 1:513], ALU.subtract,
                                [('buf', ak), ('bufc', ak)], [('Ab', m % 2)])

                    def st2(m):
                        bank = 2 + m % 2
                        for s in range(4):
                            self.mm(self.psb[bank][:, s * 128:(s + 1) * 128], Ab[m % 2][:, s * 128:(s + 1) * 128], self.identb[:], True, True,
                                    [('Ab', m % 2), 'identb'], [('ps', bank)], tr=True)
                        self.cp('dve', AT[m % 2][:], self.psb[bank][:, 0:512], [('ps', bank)], [('AT', m % 2)])

                    def st3(m):
                        i, n = steps[m]
                        ip = i - n
                        ob = 4 + i % 2
                        for s in range(4):
                            self.mm(self.ps[ob][:, 0:128], AT[m % 2][:, s * 128:(s + 1) * 128], VVk[:, s, ip * 128:(ip + 1) * 128],
                                    (n == 0 and s == 0), (n == i and s == 3), [('AT', m % 2)] + VVr, [('ps', ob)])
                        if n == i:
                            self.cp('act', merged[:, i, 1024 + h * 128:1024 + (h + 1) * 128], self.ps[ob][:, 0:128], [('ps', ob)],
                                    [('merged', i, 8 + h)])

                    for m in range(NS + 3):
                        if m < NS:
                            st0(m)
                        if 0 <= m - 1 < NS:
                            st1(m - 1)
                        if 0 <= m - 2 < NS:
                            st2(m - 2)
                        if 0 <= m - 3 < NS:
                            st3(m - 3)
            if 'merged' in self.dbg_out and l == 0:
                self.S.drain()
                self.dma(self.dbg_out['merged'], merged[:], [], ['dbgm'])
            self.flush()

    def phase_C2(self, l, mT, xsrc):
        nc = self.nc
        NB, NT = self.NB, self.NT
        W = self.I['w_out'][l]
        with ExitStack() as es:
            sb = lambda n, s, dt: es.enter_context(nc.sbuf_tensor(f"C2{l}_{n}", s, dt))
            wst = [sb(f"wst{k}", [128, KC, 256], F32) for k in range(2)]
            wb = [sb(f"wb{k}", [128, KC, 256], BF16) for k in range(2)]
            xin = [sb(f"xin{k}", [128, 256], F32) for k in range(3)]
            xo = [sb(f"xo{k}", [128, 256], F32) for k in range(3)]
            self.wload(W, KC, 0, 256, wst[0], wb[0], 0, 'w')
            n = 0
            for cg in range(8):
                slot = cg % 2
                if cg + 1 < 8:
                    self.wload(W, KC, (cg + 1) * 256, 256, wst[(cg + 1) % 2], wb[(cg + 1) % 2], (cg + 1) % 2, 'w')
                for lb in range(NB):
                    bank = n % 4
                    k3 = n % 3
                    n += 1
                    self.dma(xin[k3][:], xsrc[lb * 128:(lb + 1) * 128, cg * 256:(cg + 1) * 256], [], [('xin', k3)])
                    for kc in range(KC):
                        self.mm(self.ps[bank][:, 0:256], mT[:, kc, lb * 128:(lb + 1) * 128], wb[slot][:, kc, :], kc == 0, kc == KC - 1,
                                [('w', slot), 'mT'], [('ps', bank)])
                    self.tt('dve', xo[k3][:], self.ps[bank][:, 0:256], xin[k3][:], ALU.add, [('ps', bank), ('xin', k3)], [('xo', k3)])
                    self.dma(self.xa_d[lb * 128:(lb + 1) * 128, cg * 256:(cg + 1) * 256], xo[k3][:], [('xo', k3)], [('xa_d', lb, cg)])
            if 'xa' in self.dbg_out and l == 0:
                self.S.drain()
                self.dma(self.dbg_out['xa'], self.xa_d, [], ['dbgxa'])
            self.flush()

    def phase_halo(self, l, halo_s):
        nc = self.nc
        NB = self.NB
        par = l % 2
        rows = 4 * NB * 2
        with ExitStack() as es:
            Hall = es.enter_context(nc.sbuf_tensor(f"Hall{l}", [128, D], BF16))
            self.dma(Hall[0:rows, :], self.gout3[par], [], ['Hall'])
            for kc in range(KC):
                bank = kc % 2
                self.mm(self.ps[bank][:, 0:2 * NB], Hall[0:rows, kc * 128:(kc + 1) * 128], self.selb[0:rows, :], True, True,
                        ['Hall', 'selb'], [('ps', bank)])
                self.cp('dve', halo_s[:, kc].rearrange("p a b -> p (a b)"), self.ps[bank][:, 0:2 * NB], [('ps', bank)], [('halo', kc)])
            self.flush()

    def phase_FFN(self, l, halo_s, xdst):
        nc = self.nc
        NB, NT = self.NB, self.NT
        I = self.I
        Wu = I['w_up'][l]
        Wd = I['w_down'][l]
        TB = min(8, NB)
        ntile = NB // TB
        cts = []
        b = 0
        while b < TB:
            nb_ = min(3, TB - b)
            cts.append((b, nb_))
            b += nb_
        with ExitStack() as es:
            sb = lambda n, s, dt: es.enter_context(nc.sbuf_tensor(f"F{l}_{n}", s, dt))
            gT = sb("gT", [128, FC, TB * 128], BF16)
            cwT = sb("cwT", [128, 2 * FC, 4], F32)
            with ExitStack() as es0:
                QC = 22
                crow = [es0.enter_context(nc.sbuf_tensor(f"F{l}_crow{k}", [4, QC * 128], F32)) for k in range(2)]
                for qd in range(2 * FC // QC):
                    k2 = qd % 2
                    bank = 6 + k2
                    self.dma(crow[k2][:], I['conv_wb'][l][:, qd * QC * 128:(qd + 1) * QC * 128], [], [('crow', k2)])
                    for c in range(QC):
                        self.mm(self.ps[bank][:, c * 4:(c + 1) * 4], crow[k2][0:4, c * 128:(c + 1) * 128], self.ident_f[0:4, 0:4], True, True,
                                [('crow', k2), 'cst_f'], [('ps', bank)], tr=True)
                    self.cp('dve', cwT[:, qd * QC:(qd + 1) * QC, :], self.ps[bank][:, 0:QC * 4].rearrange("p (a b) -> p a b", b=4),
                            [('ps', bank)], [('cwT', qd)])
                self.flush()
            for T in range(ntile):
                lb0 = T * TB
                with ExitStack() as es1:
                    sb1 = lambda n, s, dt: es1.enter_context(nc.sbuf_tensor(f"F{l}_{T}_{n}", s, dt))
                    h2T = sb1("h2T", [128, KC, TB, 130], BF16)
                    wstg = [sb1(f"wstg{k}", [128, KC, 128], F32) for k in range(2)]
                    wstv = [sb1(f"wstv{k}", [128, KC, 128], F32) for k in range(2)]
                    wbg = [sb1(f"wbg{k}", [128, KC, 128], BF16) for k in range(2)]
                    wbv = [sb1(f"wbv{k}", [128, KC, 128], BF16) for k in range(2)]
                    tg = [sb1(f"tg{k}", [128, 3, 128], F32) for k in range(2)]
                    tv = [sb1(f"tv{k}", [128, 3, 128], F32) for k in range(2)]

                    def up_load(i, slot):
                        srcg = Wu[:, i * 128:(i + 1) * 128].rearrange("(kc p) c -> p kc c", p=128)
                        srcv = Wu[:, DFF + i * 128:DFF + (i + 1) * 128].rearrange("(kc p) c -> p kc c", p=128)
                        self.dma(wstg[slot][:], srcg, [], [('wstg', slot)])
                        self.dma(wstv[slot][:], srcv, [], [('wstv', slot)])
                        self.cp('pool', wbg[slot][:], wstg[slot][:], [('wstg', slot)], [('wbg', slot)])
                        self.cp('pool', wbv[slot][:], wstv[slot][:], [('wstv', slot)], [('wbv', slot)])

                    self.dma(h2T[:], self.h2T_d[:, :, lb0:lb0 + TB, :], [], ['h2T'])
                    for kc in range(KC):
                        self.cp('pool', h2T[:, kc, :, 0:2], halo_s[:, kc, lb0:lb0 + TB, :], ['h2T'], ['h2T'])
                    up_load(0, 0)
                    epn = 0
                    for i in range(FC):
                        slot = i % 2
                        if i + 1 < FC:
                            up_load(i + 1, (i + 1) % 2)
                        for cti, (b0, nbk) in enumerate(cts):
                            ncol = nbk * 130
                            bg = (2 * epn) % 4
                            bv = (2 * epn + 1) % 4
                            e2 = epn % 2
                            epn += 1
                            for kc in range(KC):
                                self.mm(self.ps[bg][:, 0:ncol].rearrange("p (a b) -> p a b", b=130), wbg[slot][:, kc, :], h2T[:, kc, b0:b0 + nbk, :],
                                        kc == 0, kc == KC - 1, [('wbg', slot), 'h2T'], [('ps', bg)])
                            for kc in range(KC):
                                self.mm(self.ps[bv][:, 0:ncol].rearrange("p (a b) -> p a b", b=130), wbv[slot][:, kc, :], h2T[:, kc, b0:b0 + nbk, :],
                                        kc == 0, kc == KC - 1, [('wbv', slot), 'h2T'], [('ps', bv)])
                            for (bank, t_, ch, trn) in ((bg, tg[e2], i, ('tg', e2)), (bv, tv[e2], FC + i, ('tv', e2))):
                                u = self.ps[bank][:, 0:ncol].rearrange("p (a b) -> p a b", b=130)
                                to = t_[:, 0:nbk, :]
                                self.act(to, u[:, :, 2:130], AF.Identity, [('ps', bank)], [trn], bias=cwT[:, ch, 3:4], scale=cwT[:, ch, 2:3])
                                self.stt(to, u[:, :, 1:129], cwT[:, ch, 1:2], to, ALU.mult, ALU.add, [('ps', bank), trn], [trn])
                                self.stt(to, u[:, :, 0:128], cwT[:, ch, 0:1], to, ALU.mult, ALU.add, [('ps', bank), trn], [trn])
                            self.act(tg[e2][:, 0:nbk, :], tg[e2][:, 0:nbk, :], AF.Silu, [('tg', e2)], [('tg', e2)])
                            self.tt('pool', gT[:, i, b0 * 128:(b0 + nbk) * 128].rearrange("p (a b) -> p a b", b=128), tg[e2][:, 0:nbk, :], tv[e2][:, 0:nbk, :],
                                    ALU.mult, [('tg', e2), ('tv', e2)], [('gT', i)])
                    self.flush()
                self.down_proj(l, T, TB, gT, Wd, xdst)

    def down_proj(self, l, T, TB, gT, Wd, xdst):
        nc = self.nc
        lb0 = T * TB
        with ExitStack() as es:
            sb = lambda n, s, dt: es.enter_context(nc.sbuf_tensor(f"Dn{l}_{T}_{n}", s, dt))
            wst = [sb(f"wst{k}", [128, 11, 256], F32) for k in range(2)]
            wd = [sb(f"wd{k}", [128, FC, 256], BF16) for k in range(2)]
            xin = [sb(f"xin{k}", [128, 256], F32) for k in range(3)]
            xo = [sb(f"xo{k}", [128, 256], F32) for k in range(3)]
            pn = [0]

            def dload(cg, slot):
                for pc in range(4):
                    s2 = pn[0] % 2
                    pn[0] += 1
                    src = Wd[pc * 11 * 128:(pc + 1) * 11 * 128, cg * 256:(cg + 1) * 256].rearrange("(kc p) c -> p kc c", p=128)
                    self.dma(wst[s2][:], src, [], [('dwst', s2)])
                    self.cp('pool', wd[slot][:, pc * 11:(pc + 1) * 11, :], wst[s2][:], [('dwst', s2)], [('wd', slot, pc)])
            dload(0, 0)
            n = 0
            for cg in range(8):
                slot = cg % 2
                if cg + 1 < 8:
                    dload(cg + 1, (cg + 1) % 2)
                wr = [('wd', slot, pc) for pc in range(4)]
                for b in range(TB):
                    lb = lb0 + b
                    bank = 4 + n % 4
                    k3 = n % 3
                    n += 1
                    self.dma(xin[k3][:], self.xa_d[lb * 128:(lb + 1) * 128, cg * 256:(cg + 1) * 256], [], [('xin', k3)])
                    for i in range(FC):
                        self.mm(self.ps[bank][:, 0:256], gT[:, i, b * 128:(b + 1) * 128], wd[slot][:, i, :], i == 0, i == FC - 1,
                                wr + [('gT', i)], [('ps', bank)])
                    self.tt('dve', xo[k3][:], self.ps[bank][:, 0:256], xin[k3][:], ALU.add, [('ps', bank), ('xin', k3)], [('xo', k3)])
                    self.dma(xdst[lb * 128:(lb + 1) * 128, cg * 256:(cg + 1) * 256], xo[k3][:], [('xo', k3)], [('xdst', lb, cg)])
            self.flush()


def core_consts(j, NB):
    NQ = 4 * NB
    cst = np.zeros((128, 1408), np.float32)
    cst[:, 0:128] = np.eye(128, dtype=np.float32)
    cst[:, 128:256] = np.eye(128, dtype=np.float32)[::-1]
    s = np.arange(128)
    cst[:, 256:384] = (s[:, None] <= s[None, :]).astype(np.float32)
    for r in range(4):
        if r < j:
            m = np.ones((128, 128), np.float32)
        elif r == j:
            m = (s[:, None] <= s[None, :]).astype(np.float32)
        else:
            m = np.zeros((128, 128), np.float32)
        cst[:, 384 + r * 128:384 + (r + 1) * 128] = m
    for slot in range(4):
        r = 3 - slot
        if r > j:
            m = np.ones((128, 128), np.float32)
        elif r == j:
            sp = 127 - s
            m = (sp[None, :] >= s[:, None]).astype(np.float32)
        else:
            m = np.zeros((128, 128), np.float32)
        cst[:, 896 + slot * 128:896 + (slot + 1) * 128] = m
    cst2 = np.zeros((128, 256), np.float32)
    q = np.arange(NQ)
    gq = 4 * (q % NB) + q // NB
    if NQ <= 64:
        cst2[0:NQ, 0:NQ] = (gq[:, None] < gq[None, :]).astype(np.float32)
    own_g = 4 * np.arange(NB) + j
    cst2[0:NQ, 64:64 + NB] = (gq[:, None] < own_g[None, :]).astype(np.float32)
    for r in range(4):
        cst2[:, 200 + r] = -30000.0 if r > j else 0.0
    for lb in range(NB):
        for k in range(2):
            if j >= 1:
                r_, lb_ = j - 1, lb
            else:
                r_, lb_ = 3, lb - 1
            if lb_ < 0:
                continue
            cst2[r_ * NB * 2 + lb_ * 2 + k, 128 + lb * 2 + k] = 1.0
    return cst, cst2


def make_in_maps(inputs, L, NB, ncores=8):
    x = np.asarray(inputs['x'], np.float32)
    f = lambda k: np.ascontiguousarray(np.asarray(inputs[k], np.float32)[:L])
    shared = {
        'attn_norm': f('attn_norm'), 'w_in': f('w_in'), 'b_forget': f('b_forget'), 'q_norm': f('q_norm'),
        'k_norm': f('k_norm'),
        'out_norm': np.ascontiguousarray(np.concatenate([f('out_norm_fox'), f('out_norm_sb')], axis=1)),
        'w_out': f('w_out'), 'ffn_norm': f('ffn_norm'), 'w_up': f('w_up'),
        'conv_wb': np.ascontiguousarray(np.concatenate([f('conv_w'), f('conv_b')[:, None, :]], axis=1)),
        'w_down': f('w_down'),
    }
    maps = []
    for c in range(ncores):
        b, j = c // 4, c % 4
        rows = np.concatenate([np.arange((4 * lb + j) * 128, (4 * lb + j + 1) * 128) for lb in range(NB)])
        cst, cst2 = core_consts(j, NB)
        m = dict(shared)
        m['x'] = np.ascontiguousarray(x[b][rows])
        m['cst'] = cst
        m['cst2'] = cst2
        maps.append(m)
    return maps


def gather_out(results, B, S, NB, ncores=8):
    out = np.zeros((B, S, D), np.float32)
    for c in range(ncores):
        b, j = c // 4, c % 4
        yc = np.asarray(results[c]['y'])
        for lb in range(NB):
            g = 4 * lb + j
            out[b, g * 128:(g + 1) * 128] = yc[lb * 128:(lb + 1) * 128]
    return out


_CACHE = {}


def kernel(**inputs):
    L = int(np.asarray(inputs['w_in']).shape[0])
    Bn, S, _ = np.asarray(inputs['x']).shape
    NB = S // 512
    key = (L, NB)
    if key not in _CACHE:
        _CACHE[key] = Builder(L, NB).build()
    nc = _CACHE[key]
    in_maps = make_in_maps(inputs, L, NB)
    res = run_bass_kernel_spmd(nc, in_maps, core_ids=list(range(8)))
    return gather_out(res.results, Bn, S, NB)
```

```python
import numpy as np
from contextlib import ExitStack
import concourse.bass as bass
import concourse.mybir as mybir
from concourse.bass_utils import run_bass_kernel_spmd

F32 = mybir.dt.float32
BF16 = mybir.dt.bfloat16
AF = mybir.ActivationFunctionType
ALU = mybir.AluOpType

D = 2048
HD = 128
NH = 8
DFF = 5632
INC = 6152
KC = D // 128
FC = DFF // 128
EPS = 1e-6
SCALE = HD ** -0.5

ENGS = ['pe', 'act', 'dve', 'pool', 'sp']
EPOCH = 20000
DMA_POOL = 12
SAME_ENGINE_SYNC = True


class Tok:
    __slots__ = ('sem', 'val', 'eng')

    def __init__(self, sem, val, eng):
        self.sem = sem
        self.val = val
        self.eng = eng


class Sched:
    def __init__(self, nc, sems):
        self.nc = nc
        self.sem_iter = iter(sems)
        self.ops = {e: [] for e in ENGS}
        self.prog = {e: [] for e in ENGS}
        self.count = {e: 0 for e in ENGS}
        self.known = {e: {} for e in ENGS}
        self.res = {}
        self.dma_sems = {}
        self.dma_cnt = {}
        self.dma_n = {}
        self.cc_sem = None
        self.cc_cnt = 0

    def _new_sem(self):
        return next(self.sem_iter)

    def _need(self, eng, tok, waits):
        if tok is None:
            return
        if tok.eng == 'pe' and eng == 'pe':
            return
        if (not SAME_ENGINE_SYNC) and tok.eng == eng:
            return
        k = self.known[eng]
        sid = id(tok.sem)
        if k.get(sid, 0) >= tok.val:
            return
        k[sid] = tok.val
        for i, (s, v) in enumerate(waits):
            if s is tok.sem:
                waits[i] = (s, max(v, tok.val))
                return
        waits.append((tok.sem, tok.val))

    def _deps(self, eng, reads, writes, waits):
        for r in reads:
            st = self.res.get(r)
            if st is not None:
                self._need(eng, st[0], waits)
        for w in writes:
            st = self.res.get(w)
            if st is not None:
                self._need(eng, st[0], waits)
                for t in st[1].values():
                    self._need(eng, t, waits)

    def _commit(self, tok, reads, writes):
        for r in reads:
            st = self.res.get(r)
            if st is None:
                st = [None, {}]
                self.res[r] = st
            st[1][id(tok.sem)] = tok
        for w in writes:
            self.res[w] = [tok, {}]

    def op(self, eng, fn, reads=(), writes=()):
        waits = []
        self._deps(eng, reads, writes, waits)
        n = self.count[eng]
        ep, k = divmod(n, EPOCH)
        while len(self.prog[eng]) <= ep:
            self.prog[eng].append(self._new_sem())
        sem = self.prog[eng][ep]
        tok = Tok(sem, k + 1, eng)
        self.count[eng] = n + 1
        self.ops[eng].append((waits, fn, sem, 1))
        self._commit(tok, reads, writes)
        return tok

    def dma(self, q, fn, reads=(), writes=(), ndma=1):
        if q not in self.dma_sems:
            self.dma_sems[q] = [self._new_sem() for _ in range(DMA_POOL)]
            self.dma_cnt[q] = [0] * DMA_POOL
            self.dma_n[q] = 0
        waits = []
        self._deps(q, reads, writes, waits)
        slot = self.dma_n[q] % DMA_POOL
        self.dma_n[q] += 1
        sem = self.dma_sems[q][slot]
        prev = self.dma_cnt[q][slot]
        if prev > 0:
            self._need(q, Tok(sem, prev, 'dma'), waits)
        newv = prev + 16 * ndma
        self.dma_cnt[q][slot] = newv
        tok = Tok(sem, newv, 'dma')
        self.ops[q].append((waits, fn, sem, -ndma))
        self._commit(tok, reads, writes)
        return tok

    def collective(self, fn, reads, writes):
        if self.cc_sem is None:
            self.cc_sem = self._new_sem()
        waits = []
        self._deps('pool', reads, writes, waits)
        self.cc_cnt += 1
        tok = Tok(self.cc_sem, self.cc_cnt, 'cc')
        self.ops['pool'].append((waits, fn, self.cc_sem, 1))
        self._commit(tok, reads, writes)
        return tok

    def drain(self):
        toks = []
        for st in self.res.values():
            if st[0] is not None:
                toks.append(st[0])
            toks.extend(st[1].values())
        for e in ENGS:
            waits = []
            for t in toks:
                if t.eng == 'pe' and e == 'pe':
                    continue
                k = self.known[e]
                sid = id(t.sem)
                if k.get(sid, 0) >= t.val:
                    continue
                k[sid] = t.val
                for i, (s, v) in enumerate(waits):
                    if s is t.sem:
                        waits[i] = (s, max(v, t.val))
                        break
                else:
                    waits.append((t.sem, t.val))
            if waits:
                self.ops[e].append((waits, None, None, 0))
        self.res = {}

    def emit(self, block):
        names = {'pe': 'tensor', 'act': 'scalar', 'dve': 'vector', 'pool': 'gpsimd', 'sp': 'sync'}

        def make(e):
            lst = self.ops[e]

            def body(eng):
                for waits, fn, sem, inc in lst:
                    for s, v in waits:
                        eng.wait_ge(s, v)
                    if fn is None:
                        continue
                    if inc > 0:
                        fn(eng).then_inc(sem, 1)
                    else:
                        cnt = [0]

                        def incf(ins, sem=sem, cnt=cnt):
                            cnt[0] += 1
                            return ins.then_inc(sem, 16)
                        fn(eng, incf)
                        assert cnt[0] == -inc, (cnt[0], inc)
            return body

        for e in ENGS:
            if self.ops[e]:
                getattr(block, names[e])(make(e))
        self.ops = {e: [] for e in ENGS}


class Builder:
    def __init__(self, L, NB, dbg=(), ncores=8):
        self.ncores = ncores
        self.rg = [[0, 1, 2, 3], [4, 5, 6, 7]] if ncores == 8 else [[0, 1, 2, 3]]
        self.L = L
        self.NB = NB
        self.NT = NB * 128
        self.dbg = set(dbg)
        NT = self.NT
        self.KTF_OFF = 0
        self.VF_OFF = 8 * NT
        self.KTS_OFF = self.VF_OFF + 8 * NB * 129
        self.VS_OFF = self.KTS_OFF + 8 * NT
        self.CW = self.VS_OFF + 8 * NB * 128
        self.NQ = 4 * NB

    def mm(self, out, lhsT, rhs, start, stop, r, w, tr=False):
        if tr:
            self.S.op('pe', lambda e: e.matmul(out, lhsT=lhsT, rhs=rhs, start=start, stop=stop, is_transpose=True), r, w)
        else:
            self.S.op('pe', lambda e: e.matmul(out, lhsT=lhsT, rhs=rhs, start=start, stop=stop), r, w)

    def act(self, out, in_, func, r, w, bias=None, scale=None, accum=None):
        kw = {}
        if bias is not None:
            kw['bias'] = bias
        if scale is not None:
            kw['scale'] = scale
        if accum is not None:
            kw['accum_out'] = accum
        self.S.op('act', lambda e: e.activation(out=out, in_=in_, func=func, **kw), r, w)

    def ts(self, eng, out, in0, s1, s2, op0, op1, r, w):
        if op1 is None:
            self.S.op(eng, lambda e: e.tensor_scalar(out=out, in0=in0, scalar1=s1, scalar2=None, op0=op0), r, w)
        else:
            self.S.op(eng, lambda e: e.tensor_scalar(out=out, in0=in0, scalar1=s1, scalar2=s2, op0=op0, op1=op1), r, w)

    def tt(self, eng, out, in0, in1, op, r, w):
        self.S.op(eng, lambda e: e.tensor_tensor(out=out, in0=in0, in1=in1, op=op), r, w)

    def stt(self, out, in0, scalar, in1, op0, op1, r, w):
        self.S.op('dve', lambda e: e.scalar_tensor_tensor(out=out, in0=in0, scalar=scalar, in1=in1, op0=op0, op1=op1), r, w)

    def cp(self, eng, out, in_, r, w):
        if eng == 'act':
            self.S.op('act', lambda e: e.copy(out=out, in_=in_), r, w)
        else:
            self.S.op(eng, lambda e: e.tensor_copy(out=out, in_=in_), r, w)

    def memset(self, eng, ap, val, w):
        self.S.op(eng, lambda e: e.memset(ap, val), (), w)

    def dma(self, out, in_, r, w, q='sp'):
        self.S.dma(q, lambda e, inc: inc(e.dma_start(out=out, in_=in_)), r, w)

    def rstd_from_ssq(self, ssq, tmp, n, r, w):
        self.ts('dve', tmp, ssq, 1.0 / n, EPS, ALU.mult, ALU.add, r, w)
        self.act(tmp, tmp, AF.Sqrt, w, w)
        self.S.op('dve', lambda e: e.reciprocal(out=tmp, in_=tmp), w, w)

    def build(self):
        L, NB, NT = self.L, self.NB, self.NT
        nc = bass.Bass("TRN2", target_bir_lowering=False)
        self.nc = nc
        dt_in = lambda n, s, dt=F32: nc.dram_tensor(n, s, dt, kind="ExternalInput").ap()
        I = {}
        I['x'] = dt_in("x", [NT, D])
        I['attn_norm'] = dt_in("attn_norm", [L, D])
        I['w_in'] = dt_in("w_in", [L, D, INC])
        I['b_forget'] = dt_in("b_forget", [L, NH])
        I['q_norm'] = dt_in("q_norm", [L, HD])
        I['k_norm'] = dt_in("k_norm", [L, HD])
        I['out_norm'] = dt_in("out_norm", [L, D])
        I['w_out'] = dt_in("w_out", [L, D, D])
        I['ffn_norm'] = dt_in("ffn_norm", [L, D])
        I['w_up'] = dt_in("w_up", [L, D, 2 * DFF])
        I['conv_wb'] = dt_in("conv_wb", [L, 4, 2 * DFF])
        I['w_down'] = dt_in("w_down", [L, DFF, D])
        I['cst'] = dt_in("cst", [128, 1408])
        I['cst2'] = dt_in("cst2", [128, 256])
        self.I = I
        y = nc.dram_tensor("y", [NT, D], F32, kind="ExternalOutput").ap()
        self.y = y
        dti = lambda n, s, dt: nc.dram_tensor(n, s, dt).ap()
        self.xa_d = dti("xa_d", [NT, D], F32)
        self.xres_d = dti("xres_d", [NT, D], F32)
        self.qT_d = dti("qT_d", [16, 128, NT], BF16)
        self.h2T_d = dti("h2T_d", [128, KC, NB, 130], BF16)
        self.g_in = {}
        self.g_out = {}
        for p in range(2):
            for hp in range(4):
                for nm, w in (('kf', 2 * NT), ('ks', 2 * NT), ('vs', 2 * NB * 128)):
                    self.g_in[(nm, p, hp)] = dti(f"gi_{nm}{p}_{hp}", [128, w], BF16)
                    self.g_out[(nm, p, hp)] = dti(f"go_{nm}{p}_{hp}", [512, w], BF16)
            for h in range(8):
                self.g_in[('vf', p, h)] = dti(f"gi_vf{p}_{h}", [128, NB * 129], BF16)
                self.g_out[('vf', p, h)] = dti(f"go_vf{p}_{h}", [512, NB * 129], BF16)
        self.gin2 = [dti(f"gin2_{p}", [129, NB * 8], F32) for p in range(2)]
        self.gout2 = [dti(f"gout2_{p}", [4 * 129, NB * 8], F32) for p in range(2)]
        self.gin3 = [dti(f"gin3_{p}", [NB * 2, D], BF16) for p in range(2)]
        self.gout3 = [dti(f"gout3_{p}", [4 * NB * 2, D], BF16) for p in range(2)]
        self.dbg_out = {}
        if 'merged' in self.dbg:
            self.dbg_out['merged'] = nc.dram_tensor("dbg_merged", [128, NB, D], BF16, kind="ExternalOutput").ap()
        if 'xa' in self.dbg:
            self.dbg_out['xa'] = nc.dram_tensor("dbg_xa", [NT, D], F32, kind="ExternalOutput").ap()

        with ExitStack() as es:
            sems = [es.enter_context(nc.semaphore(f"s{i}")) for i in range(64)]
            self.S = Sched(nc, sems)
            self.ps = [es.enter_context(nc.psum_tensor(f"ps{k}", [128, 512], F32)) for k in range(8)]
            self.psb = [p[:].bitcast(BF16) for p in self.ps]
            sb = lambda n, s, dt: es.enter_context(nc.sbuf_tensor("c_" + n, s, dt))
            self.cst_f = sb("cst_f", [128, 1408], F32)
            self.identb = sb("identb", [128, 128], BF16)
            self.Jb = sb("Jb", [128, 128], BF16)
            self.onesb = sb("onesb", [128, 128], BF16)
            self.mfox = sb("mfox", [128, 512], BF16)
            self.cst2 = sb("cst2", [128, 256], F32)
            self.selb = sb("selb", [128, 2 * NB], BF16)
            self.zeros = sb("zeros", [128, 512], F32)
            self.ones_f = sb("ones_f", [128, 128], F32)
            self.tri = self.cst_f[:, 256:384]
            self.nmsb = self.cst_f[:, 896:1408]
            self.ident_f = self.cst_f[:, 0:128]
            self.phase_consts()
            for l in range(L):
                self.layer(l)
        return nc

    def run_block(self, fn):
        import os
        self._nph = getattr(self, '_nph', 0) + 1
        if self._nph > int(os.environ.get('MAXPH', '1000')):
            return
        fn()
        self.flush()

    def flush(self):
        if not any(self.S.ops[e] for e in ENGS):
            return
        self.S.drain()
        with self.nc.Block() as block:
            self.S.emit(block)

    def phase_consts(self):
        def f():
            I = self.I
            self.dma(self.cst_f[:, 0:384], I['cst'][:, 0:384], (), ['cst_f'])
            self.dma(self.cst_f[:, 384:1408], I['cst'][:, 384:1408], (), ['cst_f2'])
            self.dma(self.cst2[:], I['cst2'], (), ['cst2'])
            self.cp('dve', self.identb[:], self.cst_f[:, 0:128], ['cst_f'], ['identb'])
            self.cp('dve', self.Jb[:], self.cst_f[:, 128:256], ['cst_f'], ['Jb'])
            self.cp('dve', self.mfox[:], self.cst_f[:, 384:896], ['cst_f2'], ['mfox'])
            self.cp('dve', self.selb[:], self.cst2[:, 128:128 + 2 * self.NB], ['cst2'], ['selb'])
            self.memset('pool', self.onesb[:], 1.0, ['onesb'])
            self.memset('pool', self.zeros[:], 0.0, ['zeros'])
            self.memset('pool', self.ones_f[:], 1.0, ['ones_f'])
        self.run_block(f)

    def layer(self, l):
        nc = self.nc
        NB, NT = self.NB, self.NT
        par = l % 2
        xsrc = self.I['x'] if l == 0 else self.xres_d
        xdst = self.y if l == self.L - 1 else self.xres_d
        with ExitStack() as es1:
            hT = es1.enter_context(nc.sbuf_tensor(f"hT{l}", [128, KC, NT], BF16))
            self.run_block(lambda: self.phase_norm_T(l, xsrc, self.I['attn_norm'][l], hT, 'A1', halo=None))
            self.run_block(lambda: self.phase_A2(l, hT))
        def g():
            S = self.S
            pass
        self.run_block(g)
        with ExitStack() as es2:
            merged = es2.enter_context(nc.sbuf_tensor(f"merged{l}", [128, NB, D], BF16))
            self.run_block(lambda: self.phase_B(l, merged))
            mT = es2.enter_context(nc.sbuf_tensor(f"mT{l}", [128, KC, NT], BF16))
            self.run_block(lambda: self.phase_norm_T(l, None, self.I['out_norm'][l], mT, 'C1', halo=None, merged=merged))
            self.run_block(lambda: self.phase_C2(l, mT, xsrc))
        with ExitStack() as es3:
            halo_s = es3.enter_context(nc.sbuf_tensor(f"halo{l}", [128, KC, NB, 2], BF16))
            self.run_block(lambda: self.phase_norm_T(l, self.xa_d, self.I['ffn_norm'][l], None, 'C3', halo=par))
            def g3():
                rg = self.rg
                gi, go = self.gin3[par], self.gout3[par]
                self.S.collective(lambda e: e.collective_compute("AllGather", ALU.bypass, replica_groups=rg, ins=[gi], outs=[go]), [], ['gout3'])
            self.run_block(g3)
            self.run_block(lambda: self.phase_halo(l, halo_s))
            self.run_block(lambda: self.phase_FFN(l, halo_s, xdst))

    def phase_norm_T(self, l, xsrc, gvec, dstT, tag, halo, merged=None):
        nc = self.nc
        NB = self.NB
        with ExitStack() as es:
            sb = lambda n, s, dt: es.enter_context(nc.sbuf_tensor(f"{tag}{l}_{n}", s, dt))
            gB = sb("gB", [128, D], F32)
            hb = [sb(f"hb{k}", [128, D], BF16) for k in range(2)]
            junk = sb("junk", [128, D], BF16)
            ssq = sb("ssq", [128, 2 * NB], F32)
            rstd = sb("rstd", [128, 2 * NB], F32)
            xb = None
            if merged is None:
                xb = [sb(f"xb{k}", [128, D], F32) for k in range(2)]
            hts = None
            if tag == 'C3':
                hts = [sb(f"hts{k}", [128, KC, 4, 130], BF16) for k in range(2)]
            self.dma(gB[:], gvec.partition_broadcast(128), (), ['gB'])
            for lb in range(NB):
                b2 = lb % 2
                if merged is None:
                    self.dma(xb[b2][:], xsrc[lb * 128:(lb + 1) * 128, :], [], [('xb', b2)])
                    self.act(junk[:], xb[b2][:], AF.Square, [('xb', b2)], ['junk', ('ssq', lb)], accum=ssq[:, lb:lb + 1])
                    self.rstd_from_ssq(ssq[:, lb:lb + 1], rstd[:, lb:lb + 1], D, [('ssq', lb)], [('rstd', lb)])
                    self.stt(hb[b2][:], xb[b2][:], rstd[:, lb:lb + 1], gB[:], ALU.mult, ALU.mult,
                             [('xb', b2), ('rstd', lb), 'gB'], [('hb', b2)])
                else:
                    for g in range(2):
                        self.act(junk[:, 0:1024], merged[:, lb, g * 1024:(g + 1) * 1024], AF.Square, ['merged'],
                                 ['junk', ('ssq', lb)], accum=ssq[:, 2 * lb + g:2 * lb + g + 1])
                    self.rstd_from_ssq(ssq[:, 2 * lb:2 * lb + 2], rstd[:, 2 * lb:2 * lb + 2], 1024, [('ssq', lb)], [('rstd', lb)])
                    for g in range(2):
                        self.stt(hb[b2][:, g * 1024:(g + 1) * 1024], merged[:, lb, g * 1024:(g + 1) * 1024],
                                 rstd[:, 2 * lb + g:2 * lb + g + 1], gB[:, g * 1024:(g + 1) * 1024], ALU.mult, ALU.mult,
                                 ['merged', ('rstd', lb), 'gB'], [('hb', b2)])
                if tag == 'C3':
                    self.dma(self.gin3[halo][2 * lb:2 * lb + 2, :], hb[b2][126:128, :], [('hb', b2)], [('gin3', lb)])
                for half in range(2):
                    bank = 2 * (lb % 2) + half
                    for k8 in range(8):
                        kc = half * 8 + k8
                        self.mm(self.psb[bank][:, k8 * 128:(k8 + 1) * 128], hb[b2][:, kc * 128:(kc + 1) * 128], self.identb[:],
                                True, True, [('hb', b2), 'identb'], [('ps', bank)], tr=True)
                    src = self.psb[bank][:, 0:1024].rearrange("p (a b) -> p a b", a=8)
                    eng = 'act' if half == 0 else 'dve'
                    if tag == 'C3':
                        s4 = (lb // 4) % 2
                        self.cp(eng, hts[s4][:, half * 8:half * 8 + 8, lb % 4, 2:130], src, [('ps', bank)], [('hts', s4, half, lb % 4)])
                    else:
                        self.cp(eng, dstT[:, half * 8:half * 8 + 8, lb * 128:(lb + 1) * 128], src, [('ps', bank)], [('dstT', lb, half)])
                if tag == 'C3' and (lb % 4 == 3 or lb == NB - 1):
                    s4 = (lb // 4) % 2
                    nb4 = lb % 4 + 1
                    lb0 = lb - (nb4 - 1)
                    self.dma(self.h2T_d[:, :, lb0:lb0 + nb4, :], hts[s4][:, :, 0:nb4, :],
                             [('hts', s4, h, q) for h in range(2) for q in range(nb4)], [('h2T_d', lb0)])
            self.flush()

    def wload(self, wsrc, rows_kc, c0, ncols, wst, wb, slot, rname, row0=0):
        src = wsrc[row0:row0 + rows_kc * 128, c0:c0 + ncols].rearrange("(kc p) c -> p kc c", p=128)
        self.dma(wst[:, 0:rows_kc, 0:ncols], src, [], [(rname + 'st', slot)])
        self.cp('pool', wb[:, 0:rows_kc, 0:ncols], wst[:, 0:rows_kc, 0:ncols], [(rname + 'st', slot)], [(rname, slot)])

    def gather(self, key, reads):
        rg = self.rg
        a, b = self.g_in[key], self.g_out[key]
        self.S.collective(lambda e: e.collective_compute("AllGather", ALU.bypass, replica_groups=rg, ins=[a], outs=[b]), reads, [('gout', key)])

    def phase_A2(self, l, hT):
        nc = self.nc
        NB, NT = self.NB, self.NT
        I = self.I
        par = l % 2
        gin2 = self.gin2[par]
        GI = self.g_in
        W = I['w_in'][l]
        NTT = (NT + 511) // 512
        with ExitStack() as es:
            sb = lambda n, s, dt: es.enter_context(nc.sbuf_tensor(f"A2{l}_{n}", s, dt))
            wst = [sb(f"wst{k}", [128, KC, 256], F32) for k in range(2)]
            wb = [sb(f"wb{k}", [128, KC, 256], BF16) for k in range(2)]
            ost = [sb(f"ost{k}", [128, NT], BF16) for k in range(2)]
            vst = [sb(f"vst{k}", [128, 2, NB, 129], BF16) for k in range(2)]
            sqb = [sb(f"sqb{k}", [128, 512], BF16) for k in range(2)]
            rs = [sb(f"rs{k}", [128, 512], F32) for k in range(2)]
            tm = [sb(f"tm{k}", [128, 256], BF16) for k in range(2)]
            gq = sb("gq", [128, 2], F32)
            wfst = sb("wfst", [128, KC, 8], F32)
            wfb = sb("wfb", [128, KC, 8], BF16)
            bfB = sb("bfB", [128, 8], F32)
            lg = sb("lg", [128, NB, 8], F32)
            cwst = sb("cwst", [128, NB * 8], F32)
            for k in range(2):
                self.memset('pool', vst[k][:, :, :, 128:129], 1.0, [('vst', k)])
            self.dma(gq[:, 0:1], I['q_norm'][l].rearrange("(p o) -> p o", o=1), [], ['gq0'])
            self.dma(gq[:, 1:2], I['k_norm'][l].rearrange("(p o) -> p o", o=1), [], ['gq1'])
            self.dma(wfst[:], W[:, 3072:3080].rearrange("(kc p) c -> p kc c", p=128), [], ['wfst'])
            self.dma(bfB[:], I['b_forget'][l].partition_broadcast(128), [], ['bfB'])
            self.cp('pool', wfb[:], wfst[:], ['wfst'], ['wfb'])

            groups = []
            for gI in range(4):
                groups.append((gI * 256, 'fq', 2 * gI))
            for gI in range(4):
                groups.append((1024 + gI * 256, 'fk', 2 * gI))
            for gI in range(4):
                groups.append((2048 + gI * 256, 'fv', 2 * gI))
            for gI in range(4):
                groups.append((3080 + gI * 256, 'sq', 2 * gI))
            for gI in range(4):
                groups.append((4104 + gI * 256, 'sk', 2 * gI))
            for gI in range(4):
                groups.append((5128 + gI * 256, 'sv', 2 * gI))
            self.wload(W, KC, groups[0][0], 256, wst[0], wb[0], 0, 'w')
            ostn = 0
            bankrr = 0
            for n, (c0, kind, h0) in enumerate(groups):
                slot = n % 2
                if n + 1 < len(groups):
                    self.wload(W, KC, groups[n + 1][0], 256, wst[(n + 1) % 2], wb[(n + 1) % 2], (n + 1) % 2, 'w')
                if kind in ('fq', 'fk', 'sq'):
                    for hh in range(2):
                        head = h0 + hh
                        o = ost[ostn % 2]
                        orn = ('ost', ostn % 2)
                        ostn += 1
                        for tt in range(NTT):
                            t0 = tt * 512
                            tn = min(512, NT - t0)
                            bank = bankrr % 4
                            bankrr += 1
                            for kc in range(KC):
                                self.mm(self.ps[bank][:, 0:tn], wb[slot][:, kc, hh * 128:(hh + 1) * 128], hT[:, kc, t0:t0 + tn],
                                        kc == 0, kc == KC - 1, [('w', slot), 'hT'], [('ps', bank)])
                            if kind == 'sq':
                                self.cp('act', o[:, t0:t0 + tn], self.ps[bank][:, 0:tn], [('ps', bank)], [orn])
                            else:
                                k2 = tt % 2
                                gcol = gq[:, 0:1] if kind == 'fq' else gq[:, 1:2]
                                self.act(sqb[k2][:, 0:tn], self.ps[bank][:, 0:tn], AF.Square, [('ps', bank)], [('sqb', k2)])
                                sb_ = 4 + k2
                                self.mm(self.ps[sb_][:, 0:tn], self.onesb[:], sqb[k2][:, 0:tn], True, True,
                                        [('sqb', k2), 'onesb'], [('ps', sb_)])
                                self.ts('dve', rs[k2][:, 0:tn], self.ps[sb_][:, 0:tn], 1.0 / HD, EPS, ALU.mult, ALU.add,
                                        [('ps', sb_)], [('rs', k2)])
                                self.act(rs[k2][:, 0:tn], rs[k2][:, 0:tn], AF.Sqrt, [('rs', k2)], [('rs', k2)])
                                self.S.op('dve', (lambda a: (lambda e: e.reciprocal(out=a, in_=a)))(rs[k2][:, 0:tn]), [('rs', k2)], [('rs', k2)])
                                self.stt(o[:, t0:t0 + tn], self.ps[bank][:, 0:tn], gcol, rs[k2][:, 0:tn], ALU.mult, ALU.mult,
                                         [('ps', bank), ('rs', k2), 'gq0', 'gq1'], [orn])
                        if kind == 'fq':
                            self.dma(self.qT_d[head], o[:], [orn], [('qT_d', head)])
                        elif kind == 'sq':
                            self.dma(self.qT_d[8 + head], o[:], [orn], [('qT_d', 8 + head)])
                        else:
                            self.dma(GI[('kf', par, head // 2)][:, (head % 2) * NT:(head % 2 + 1) * NT], o[:], [orn], [('gin', 'kf', head)])
                            if hh == 1:
                                self.gather(('kf', par, head // 2), [('gin', 'kf', head - 1), ('gin', 'kf', head)])
                elif kind == 'fv':
                    v = vst[(n // 1) % 2]
                    vrn = ('vst', n % 2)
                    for lb in range(NB):
                        bank = bankrr % 4
                        bankrr += 1
                        for kc in range(KC):
                            self.mm(self.ps[bank][:, 0:256], hT[:, kc, lb * 128:(lb + 1) * 128], wb[slot][:, kc, :],
                                    kc == 0, kc == KC - 1, [('w', slot), 'hT'], [('ps', bank)])
                        self.cp('act', v[:, :, lb, 0:128], self.ps[bank][:, 0:256].rearrange("p (a b) -> p a b", a=2),
                                [('ps', bank)], [vrn])
                    for hh in range(2):
                        head = h0 + hh
                        self.dma(GI[('vf', par, head)][:, :], v[:, hh].rearrange("p a b -> p (a b)"), [vrn], [('gin', 'vf', head)])
                        self.gather(('vf', par, head), [('gin', 'vf', head)])
                elif kind == 'sk':
                    o2 = [ost[ostn % 2], ost[(ostn + 1) % 2]]
                    orn2 = [('ost', ostn % 2), ('ost', (ostn + 1) % 2)]
                    ostn += 2
                    for lb in range(NB):
                        bank = bankrr % 4
                        bankrr += 1
                        for kc in range(KC):
                            self.mm(self.ps[bank][:, 0:256], hT[:, kc, lb * 128:(lb + 1) * 128], wb[slot][:, kc, :],
                                    kc == 0, kc == KC - 1, [('w', slot), 'hT'], [('ps', bank)])
                        k2 = lb % 2
                        self.cp('act', tm[k2][:], self.ps[bank][:, 0:256], [('ps', bank)], [('tm', k2)])
                        jb = 4 + k2
                        for hh in range(2):
                            self.mm(self.ps[jb][:, hh * 128:(hh + 1) * 128], tm[k2][:, hh * 128:(hh + 1) * 128], self.Jb[:], True, True,
                                    [('tm', k2), 'Jb'], [('ps', jb)])
                        for hh in range(2):
                            self.cp('dve', o2[hh][:, lb * 128:(lb + 1) * 128], self.ps[jb][:, hh * 128:(hh + 1) * 128], [('ps', jb)], [orn2[hh]])
                    for hh in range(2):
                        head = h0 + hh
                        self.dma(GI[('ks', par, head // 2)][:, (head % 2) * NT:(head % 2 + 1) * NT], o2[hh][:], [orn2[hh]], [('gin', 'ks', head)])
                    self.gather(('ks', par, h0 // 2), [('gin', 'ks', h0), ('gin', 'ks', h0 + 1)])
                elif kind == 'sv':
                    v = vst[n % 2]
                    vrn = ('vst', n % 2)
                    for lb in range(NB):
                        bank = bankrr % 4
                        bankrr += 1
                        for kc in range(KC):
                            self.mm(self.ps[bank][:, 0:256], hT[:, kc, lb * 128:(lb + 1) * 128], wb[slot][:, kc, :],
                                    kc == 0, kc == KC - 1, [('w', slot), 'hT'], [('ps', bank)])
                        k2 = lb % 2
                        self.cp('act', tm[k2][:], self.ps[bank][:, 0:256], [('ps', bank)], [('tm', k2)])
                        jb = 4 + k2
                        self.mm(self.ps[jb][:, 0:256], self.Jb[:], tm[k2][:], True, True, [('tm', k2), 'Jb'], [('ps', jb)])
                        self.cp('dve', v[:, :, lb, 0:128], self.ps[jb][:, 0:256].rearrange("p (a b) -> p a b", a=2), [('ps', jb)], [vrn])
                    for hh in range(2):
                        head = h0 + hh
                        c = (head % 2) * NB * 128
                        self.dma(GI[('vs', par, head // 2)][:, c:c + NB * 128].rearrange("p (a b) -> p a b", a=NB), v[:, hh, :, 0:128], [vrn], [('gin', 'vs', head)])
                    self.gather(('vs', par, h0 // 2), [('gin', 'vs', h0), ('gin', 'vs', h0 + 1)])
            for lb in range(NB):
                bank = 6 + lb % 2
                for kc in range(KC):
                    self.mm(self.ps[bank][:, 0:8], hT[:, kc, lb * 128:(lb + 1) * 128], wfb[:, kc, :], kc == 0, kc == KC - 1,
                            ['wfb', 'hT'], [('ps', bank)])
                self.tt('dve', lg[:, lb, :], self.ps[bank][:, 0:8], bfB[:], ALU.add, [('ps', bank), 'bfB'], [('lg', lb)])
                self.act(lg[:, lb, :], lg[:, lb, :], AF.Exp, [('lg', lb)], [('lg', lb)], scale=-1.0)
                self.act(lg[:, lb, :], lg[:, lb, :], AF.Ln, [('lg', lb)], [('lg', lb)], bias=1.0)
                self.mm(self.ps[bank][:, 16:24], self.tri, lg[:, lb, :], True, True, [('lg', lb), 'cst_f'], [('ps', bank)])
                self.cp('dve', cwst[:, lb * 8:(lb + 1) * 8], self.ps[bank][:, 16:24], [('ps', bank)], ['cwst'])
            self.dma(gin2[0:128, :], cwst[:], ['cwst'], ['gin2a'])
            self.dma(gin2[128:129, :], cwst[127:128, :], ['cwst'], ['gin2b'])
            rg = self.rg
            go2 = self.gout2[par]
            self.S.collective(lambda e: e.collective_compute("AllGather", ALU.bypass, replica_groups=rg, ins=[gin2], outs=[go2]), ['gin2a', 'gin2b'], ['gout2'])
            self.flush()

    def phase_B(self, l, merged):
        nc = self.nc
        NB, NT, NQ = self.NB, self.NT, self.NQ
        par = l % 2
        gout2 = self.gout2[par]
        GO = self.g_out
        with ExitStack() as es:
            sb = lambda n, s, dt: es.enter_context(nc.sbuf_tensor(f"B{l}_{n}", s, dt))
            KT = [sb(f"KT{k}", [128, 4, NT], BF16) for k in range(2)]
            VV = [sb(f"VV{k}", [128, 4, NB * 129], BF16) for k in range(2)]
            QT = [sb(f"QT{k}", [128, NT], BF16) for k in range(2)]
            cwS = sb("cwS", [128, 4, NB * 8], F32)
            bsT = sb("bsT", [64, 8], F32)
            bsbc = sb("bsbc", [64, 128], F32)
            Dt = sb("Dt", [128, NQ], F32)
            own = sb("own", [128, NB], F32)
            bias4 = [sb(f"bias4_{k}", [128, 4, NQ], F32) for k in range(2)]
            NPT = 6
            pT = [sb(f"pT{k}", [128, 128], BF16) for k in range(NPT)]
            rden = sb("rden", [128, 8], F32)
            NA = 4
            a_ = [sb(f"a{k}", [128, 512], F32) for k in range(NA)]
            buf = [sb(f"buf{k}", [128, 513], F32) for k in range(NA)]
            NAB = 3
            Ab = [sb(f"Ab{k}", [128, 512], BF16) for k in range(NAB)]
            AT = [sb(f"AT{k}", [128, 512], BF16) for k in range(NAB)]
            Lall = self.cst2[0:NQ, 0:NQ] if NQ <= 64 else None
            Lown = self.cst2[0:NQ, 64:64 + NB]

            self.dma(cwS[:], gout2.rearrange("(r q) c -> q r c", q=129)[0:128], [], ['cwS'])
            for r in range(4):
                self.dma(bsT[r * NB:(r + 1) * NB, :], gout2[r * 129 + 128, :].rearrange("(a b) -> a b", b=8), [], [('bsT', r)])
            bsT_r = [('bsT', r) for r in range(4)]

            def load_head(kind, h, k):
                if kind == 'f':
                    c = (h % 2) * NT
                    self.dma(KT[k][:], GO[('kf', par, h // 2)].rearrange("(r p) c -> p r c", p=128)[:, :, c:c + NT], [], [('KT', k, r) for r in range(4)])
                    self.dma(VV[k][:], GO[('vf', par, h)].rearrange("(r p) c -> p r c", p=128), [], [('VV', k, r) for r in range(4)])
                    self.dma(QT[k][:], self.qT_d[h], [], [('QT', k)])
                else:
                    c = (h % 2) * NT
                    cv = (h % 2) * NB * 128
                    gk_ = GO[('ks', par, h // 2)]
                    gv_ = GO[('vs', par, h // 2)]
                    for r in range(4):
                        self.dma(KT[k][:, 3 - r, :], gk_[r * 128:(r + 1) * 128, c:c + NT], [], [('KT', k, r)])
                        self.dma(VV[k][:, 3 - r, 0:NB * 128], gv_[r * 128:(r + 1) * 128, cv:cv + NB * 128], [], [('VV', k, r)])
                    self.dma(QT[k][:], self.qT_d[8 + h], [], [('QT', k)])

            heads = [('f', h) for h in range(NH)] + [('s', h) for h in range(NH)]
            load_head(heads[0][0], heads[0][1], 0)
            ptn = 0
            stn = 0
            for hn, (kind, h) in enumerate(heads):
                k = hn % 2
                if hn + 1 < len(heads):
                    load_head(heads[hn + 1][0], heads[hn + 1][1], (hn + 1) % 2)
                KTk, VVk, QTk = KT[k], VV[k], QT[k]
                KTr = [('KT', k, r) for r in range(4)]
                VVr = [('VV', k, r) for r in range(4)]
                if kind == 'f':
                    self.ts('dve', bsbc[0:NQ, :], self.ones_f[0:NQ, :], bsT[0:NQ, h:h + 1], None, ALU.mult, None,
                            bsT_r + ['ones_f'], ['bsbc'])
                    self.mm(self.ps[6][:, 0:NQ], bsbc[0:NQ, :], Lall, True, True, ['bsbc', 'cst2'], [('ps', 6)])
                    self.mm(self.ps[7][:, 0:NB], bsbc[0:NQ, :], Lown, True, True, ['bsbc', 'cst2'], [('ps', 7)])
                    self.tt('dve', Dt[:].rearrange("p (r a) -> p r a", r=4),
                            cwS[:].rearrange("p r (a c) -> p r a c", c=8)[:, :, :, h], self.ps[6][:, 0:NQ].rearrange("p (r a) -> p r a", r=4),
                            ALU.add, ['cwS', ('ps', 6)], ['Dt'])
                    self.cp('dve', own[:], self.ps[7][:, 0:NB], [('ps', 7)], ['own'])
                    for i0 in range(0, NB, 4):
                        i1 = min(i0 + 4, NB)
                        gk = (i0 // 4) % 2
                        for i in range(i0, i1):
                            self.ts('dve', bias4[gk][:, i - i0, :], Dt[:], own[:, i:i + 1], None, ALU.subtract, None,
                                    ['Dt', 'own'], [('bias4', gk, i - i0)])
                            dg = bias4[gk][:, i - i0, :].rearrange("p (r a) -> p r a", r=4)[:, :, i]
                            self.tt('dve', dg, dg, self.cst2[:, 200:204], ALU.add, [('bias4', gk, i - i0), 'cst2'], [('bias4', gk, i - i0)])
                        steps = [(ip, r) for ip in range(i1) for r in range(4)]

                        def qk(n):
                            ip, r = steps[n]
                            ist = max(ip, i0)
                            ncol = (i1 - ist) * 128
                            bank = n % 2
                            self.mm(self.ps[bank][:, 0:ncol], KTk[:, r, ip * 128:(ip + 1) * 128], QTk[:, ist * 128:i1 * 128], True, True,
                                    KTr + [('QT', k)], [('ps', bank)])
                        qk(0)
                        for n, (ip, r) in enumerate(steps):
                            if n + 1 < len(steps):
                                qk(n + 1)
                            ist = max(ip, i0)
                            bank = n % 2
                            q = r * NB + ip
                            for i in range(ist, i1):
                                pk = ptn % NPT
                                ptn += 1
                                self.act(pT[pk][:], self.ps[bank][:, (i - ist) * 128:(i - ist + 1) * 128], AF.Exp,
                                         [('ps', bank), ('bias4', gk, i - i0)], [('pT', pk)],
                                         bias=bias4[gk][:, i - i0, q:q + 1], scale=SCALE)
                                if ip == i:
                                    self.tt('pool', pT[pk][:], pT[pk][:], self.mfox[:, r * 128:(r + 1) * 128], ALU.mult,
                                            [('pT', pk), 'mfox'], [('pT', pk)])
                                ob = 2 + (i - i0)
                                self.mm(self.ps[ob][:, 0:129], pT[pk][:], VVk[:, r, ip * 129:ip * 129 + 129],
                                        (ip == 0 and r == 0), (ip == i and r == 3), [('pT', pk)] + VVr, [('ps', ob)])
                                if ip == i and r == 3:
                                    rc = rden[:, (i - i0):(i - i0) + 1]
                                    self.S.op('dve', (lambda a, b: (lambda e: e.reciprocal(out=a, in_=b)))(rc, self.ps[ob][:, 128:129]),
                                              [('ps', ob)], [('rden', i - i0)])
                                    self.ts('dve', merged[:, i, h * 128:(h + 1) * 128], self.ps[ob][:, 0:128], rc, None, ALU.mult, None,
                                            [('ps', ob), ('rden', i - i0)], [('merged', i, h)])
                else:
                    steps = [(i, n) for i in range(NB) for n in range(i + 1)]
                    NS = len(steps)

                    def st0(m):
                        i, n = steps[m]
                        ip = i - n
                        bank = m % 2
                        ak = m % NA
                        self.mm(self.ps[bank][:, 0:512].rearrange("p (a b) -> p a b", a=4), QTk[:, i * 128:(i + 1) * 128],
                                KTk[:, :, ip * 128:(ip + 1) * 128], True, True, KTr + [('QT', k)], [('ps', bank)])
                        self.act(a_[ak][:], self.ps[bank][:, 0:512], AF.Sigmoid, [('ps', bank)], [('a', ak)], scale=-SCALE)
                        if n == 0:
                            self.tt('dve', a_[ak][:], a_[ak][:], self.nmsb, ALU.max, [('a', ak), 'cst_f2'], [('a', ak)])

                    def st1(m):
                        i, n = steps[m]
                        ak = m % NA
                        pk_ = (m - 1) % NA
                        if n == 0:
                            ini = 1.0
                            rd = [('a', ak), 'zeros']
                            self.memset('pool', buf[ak][:, 0:1], 1.0, [('bufc', ak)])
                        else:
                            ini = buf[pk_][:, 512:513]
                            rd = [('a', ak), 'zeros', ('buf', pk_)]
                            self.cp('pool', buf[ak][:, 0:1], buf[pk_][:, 512:513], [('buf', pk_)], [('bufc', ak)])
                        self.S.op('dve', (lambda o, d0, ini_: (lambda e: e.tensor_tensor_scan(out=o, data0=d0, data1=self.zeros[:], initial=ini_,
                                                                                            op0=ALU.mult, op1=ALU.add)))(
                            buf[ak][:, 1:513], a_[ak][:], ini), rd, [('buf', ak)])
                        self.tt('pool', Ab[m % NAB][:], buf[ak][:, 0:512], buf[ak][:, 1:513], ALU.subtract,
                                [('buf', ak), ('bufc', ak)], [('Ab', m % NAB)])

                    def st2(m):
                        bank = 2 + m % 2
                        for s in range(4):
                            self.mm(self.psb[bank][:, s * 128:(s + 1) * 128], Ab[m % NAB][:, s * 128:(s + 1) * 128], self.identb[:], True, True,
                                    [('Ab', m % NAB), 'identb'], [('ps', bank)], tr=True)
                        self.cp('act', AT[m % NAB][:], self.psb[bank][:, 0:512], [('ps', bank)], [('AT', m % NAB)])

                    def st3(m):
                        i, n = steps[m]
                        ip = i - n
                        ob = 4 + i % 2
                        for s in range(4):
                            self.mm(self.ps[ob][:, 0:128], AT[m % NAB][:, s * 128:(s + 1) * 128], VVk[:, s, ip * 128:(ip + 1) * 128],
                                    (n == 0 and s == 0), (n == i and s == 3), [('AT', m % NAB)] + VVr, [('ps', ob)])
                        if n == i:
                            self.cp('dve', merged[:, i, 1024 + h * 128:1024 + (h + 1) * 128], self.ps[ob][:, 0:128], [('ps', ob)],
                                    [('merged', i, 8 + h)])

                    for m in range(NS + 3):
                        if m < NS:
                            st0(m)
                        if 0 <= m - 1 < NS:
                            st1(m - 1)
                        if 0 <= m - 2 < NS:
                            st2(m - 2)
                        if 0 <= m - 3 < NS:
                            st3(m - 3)
            if 'merged' in self.dbg_out and l == 0:
                self.S.drain()
                self.dma(self.dbg_out['merged'], merged[:], [], ['dbgm'])
            self.flush()

    def phase_C2(self, l, mT, xsrc):
        nc = self.nc
        NB, NT = self.NB, self.NT
        W = self.I['w_out'][l]
        with ExitStack() as es:
            sb = lambda n, s, dt: es.enter_context(nc.sbuf_tensor(f"C2{l}_{n}", s, dt))
            wst = [sb(f"wst{k}", [128, KC, 256], F32) for k in range(2)]
            wb = [sb(f"wb{k}", [128, KC, 256], BF16) for k in range(2)]
            xin = [sb(f"xin{k}", [128, 256], F32) for k in range(3)]
            xo = [sb(f"xo{k}", [128, 256], F32) for k in range(3)]
            self.wload(W, KC, 0, 256, wst[0], wb[0], 0, 'w')
            n = 0
            for cg in range(8):
                slot = cg % 2
                if cg + 1 < 8:
                    self.wload(W, KC, (cg + 1) * 256, 256, wst[(cg + 1) % 2], wb[(cg + 1) % 2], (cg + 1) % 2, 'w')
                for lb in range(NB):
                    bank = n % 4
                    k3 = n % 3
                    n += 1
                    self.dma(xin[k3][:], xsrc[lb * 128:(lb + 1) * 128, cg * 256:(cg + 1) * 256], [], [('xin', k3)])
                    for kc in range(KC):
                        self.mm(self.ps[bank][:, 0:256], mT[:, kc, lb * 128:(lb + 1) * 128], wb[slot][:, kc, :], kc == 0, kc == KC - 1,
                                [('w', slot), 'mT'], [('ps', bank)])
                    self.tt('dve', xo[k3][:], self.ps[bank][:, 0:256], xin[k3][:], ALU.add, [('ps', bank), ('xin', k3)], [('xo', k3)])
                    self.dma(self.xa_d[lb * 128:(lb + 1) * 128, cg * 256:(cg + 1) * 256], xo[k3][:], [('xo', k3)], [('xa_d', lb, cg)])
            if 'xa' in self.dbg_out and l == 0:
                self.S.drain()
                self.dma(self.dbg_out['xa'], self.xa_d, [], ['dbgxa'])
            self.flush()

    def phase_halo(self, l, halo_s):
        nc = self.nc
        NB = self.NB
        par = l % 2
        rows = 4 * NB * 2
        with ExitStack() as es:
            Hall = es.enter_context(nc.sbuf_tensor(f"Hall{l}", [128, D], BF16))
            self.dma(Hall[0:rows, :], self.gout3[par], [], ['Hall'])
            for kc in range(KC):
                bank = kc % 2
                self.mm(self.ps[bank][:, 0:2 * NB], Hall[0:rows, kc * 128:(kc + 1) * 128], self.selb[0:rows, :], True, True,
                        ['Hall', 'selb'], [('ps', bank)])
                self.cp('dve', halo_s[:, kc].rearrange("p a b -> p (a b)"), self.ps[bank][:, 0:2 * NB], [('ps', bank)], [('halo', kc)])
            self.flush()

    def phase_FFN(self, l, halo_s, xdst):
        nc = self.nc
        NB, NT = self.NB, self.NT
        I = self.I
        Wu = I['w_up'][l]
        Wd = I['w_down'][l]
        TB = min(8, NB)
        ntile = NB // TB
        cts = []
        b = 0
        while b < TB:
            nb_ = min(3, TB - b)
            cts.append((b, nb_))
            b += nb_
        with ExitStack() as es:
            sb = lambda n, s, dt: es.enter_context(nc.sbuf_tensor(f"F{l}_{n}", s, dt))
            gT = sb("gT", [128, FC, TB * 128], BF16)
            cwT = sb("cwT", [128, 2 * FC, 4], F32)
            with ExitStack() as es0:
                QC = 22
                crow = [es0.enter_context(nc.sbuf_tensor(f"F{l}_crow{k}", [4, QC * 128], F32)) for k in range(2)]
                for qd in range(2 * FC // QC):
                    k2 = qd % 2
                    bank = 6 + k2
                    self.dma(crow[k2][:], I['conv_wb'][l][:, qd * QC * 128:(qd + 1) * QC * 128], [], [('crow', k2)])
                    for c in range(QC):
                        self.mm(self.ps[bank][:, c * 4:(c + 1) * 4], crow[k2][0:4, c * 128:(c + 1) * 128], self.ident_f[0:4, 0:4], True, True,
                                [('crow', k2), 'cst_f'], [('ps', bank)], tr=True)
                    self.cp('dve', cwT[:, qd * QC:(qd + 1) * QC, :], self.ps[bank][:, 0:QC * 4].rearrange("p (a b) -> p a b", b=4),
                            [('ps', bank)], [('cwT', qd)])
                self.flush()
            for T in range(ntile):
                lb0 = T * TB
                with ExitStack() as es1:
                    sb1 = lambda n, s, dt: es1.enter_context(nc.sbuf_tensor(f"F{l}_{T}_{n}", s, dt))
                    h2T = sb1("h2T", [128, KC, TB, 130], BF16)
                    wstg = [sb1(f"wstg{k}", [128, KC, 128], F32) for k in range(2)]
                    wstv = [sb1(f"wstv{k}", [128, KC, 128], F32) for k in range(2)]
                    wbg = [sb1(f"wbg{k}", [128, KC, 128], BF16) for k in range(2)]
                    wbv = [sb1(f"wbv{k}", [128, KC, 128], BF16) for k in range(2)]
                    tg = [sb1(f"tg{k}", [128, 3, 128], F32) for k in range(2)]
                    tv = [sb1(f"tv{k}", [128, 3, 128], F32) for k in range(2)]

                    def up_load(i, slot):
                        srcg = Wu[:, i * 128:(i + 1) * 128].rearrange("(kc p) c -> p kc c", p=128)
                        srcv = Wu[:, DFF + i * 128:DFF + (i + 1) * 128].rearrange("(kc p) c -> p kc c", p=128)
                        self.dma(wstg[slot][:], srcg, [], [('wstg', slot)])
                        self.dma(wstv[slot][:], srcv, [], [('wstv', slot)])
                        self.cp('act', wbg[slot][:], wstg[slot][:], [('wstg', slot)], [('wbg', slot)])
                        self.cp('dve', wbv[slot][:], wstv[slot][:], [('wstv', slot)], [('wbv', slot)])

                    self.dma(h2T[:], self.h2T_d[:, :, lb0:lb0 + TB, :], [], ['h2T'])
                    for kc in range(KC):
                        self.cp('pool', h2T[:, kc, :, 0:2], halo_s[:, kc, lb0:lb0 + TB, :], ['h2T'], ['h2T'])
                    up_load(0, 0)
                    epn = 0
                    for i in range(FC):
                        slot = i % 2
                        if i + 1 < FC:
                            up_load(i + 1, (i + 1) % 2)
                        for cti, (b0, nbk) in enumerate(cts):
                            ncol = nbk * 130
                            bg = (2 * epn) % 4
                            bv = (2 * epn + 1) % 4
                            e2 = epn % 2
                            epn += 1
                            for kc in range(KC):
                                self.mm(self.ps[bg][:, 0:ncol].rearrange("p (a b) -> p a b", b=130), wbg[slot][:, kc, :], h2T[:, kc, b0:b0 + nbk, :],
                                        kc == 0, kc == KC - 1, [('wbg', slot), 'h2T'], [('ps', bg)])
                            for kc in range(KC):
                                self.mm(self.ps[bv][:, 0:ncol].rearrange("p (a b) -> p a b", b=130), wbv[slot][:, kc, :], h2T[:, kc, b0:b0 + nbk, :],
                                        kc == 0, kc == KC - 1, [('wbv', slot), 'h2T'], [('ps', bv)])
                            for (bank, t_, ch, trn) in ((bg, tg[e2], i, ('tg', e2)), (bv, tv[e2], FC + i, ('tv', e2))):
                                u = self.ps[bank][:, 0:ncol].rearrange("p (a b) -> p a b", b=130)
                                to = t_[:, 0:nbk, :]
                                self.act(to, u[:, :, 2:130], AF.Identity, [('ps', bank)], [trn], bias=cwT[:, ch, 3:4], scale=cwT[:, ch, 2:3])
                                self.stt(to, u[:, :, 1:129], cwT[:, ch, 1:2], to, ALU.mult, ALU.add, [('ps', bank), trn], [trn])
                                self.stt(to, u[:, :, 0:128], cwT[:, ch, 0:1], to, ALU.mult, ALU.add, [('ps', bank), trn], [trn])
                            self.act(tg[e2][:, 0:nbk, :], tg[e2][:, 0:nbk, :], AF.Silu, [('tg', e2)], [('tg', e2)])
                            self.tt('pool', gT[:, i, b0 * 128:(b0 + nbk) * 128].rearrange("p (a b) -> p a b", b=128), tg[e2][:, 0:nbk, :], tv[e2][:, 0:nbk, :],
                                    ALU.mult, [('tg', e2), ('tv', e2)], [('gT', i)])
                    self.flush()
                self.down_proj(l, T, TB, gT, Wd, xdst)

    def down_proj(self, l, T, TB, gT, Wd, xdst):
        nc = self.nc
        lb0 = T * TB
        with ExitStack() as es:
            sb = lambda n, s, dt: es.enter_context(nc.sbuf_tensor(f"Dn{l}_{T}_{n}", s, dt))
            wst = [sb(f"wst{k}", [128, 11, 256], F32) for k in range(2)]
            wd = [sb(f"wd{k}", [128, FC, 256], BF16) for k in range(2)]
            xin = [sb(f"xin{k}", [128, 256], F32) for k in range(3)]
            xo = [sb(f"xo{k}", [128, 256], F32) for k in range(3)]
            pn = [0]

            def dload(cg, slot):
                for pc in range(4):
                    s2 = pn[0] % 2
                    pn[0] += 1
                    src = Wd[pc * 11 * 128:(pc + 1) * 11 * 128, cg * 256:(cg + 1) * 256].rearrange("(kc p) c -> p kc c", p=128)
                    self.dma(wst[s2][:], src, [], [('dwst', s2)])
                    self.cp(('act', 'dve', 'act', 'pool')[pc], wd[slot][:, pc * 11:(pc + 1) * 11, :], wst[s2][:], [('dwst', s2)], [('wd', slot, pc)])
            dload(0, 0)
            n = 0
            for cg in range(8):
                slot = cg % 2
                if cg + 1 < 8:
                    dload(cg + 1, (cg + 1) % 2)
                wr = [('wd', slot, pc) for pc in range(4)]
                for b in range(TB):
                    lb = lb0 + b
                    bank = 4 + n % 4
                    k3 = n % 3
                    n += 1
                    self.dma(xin[k3][:], self.xa_d[lb * 128:(lb + 1) * 128, cg * 256:(cg + 1) * 256], [], [('xin', k3)])
                    for i in range(FC):
                        self.mm(self.ps[bank][:, 0:256], gT[:, i, b * 128:(b + 1) * 128], wd[slot][:, i, :], i == 0, i == FC - 1,
                                wr + [('gT', i)], [('ps', bank)])
                    self.tt('dve', xo[k3][:], self.ps[bank][:, 0:256], xin[k3][:], ALU.add, [('ps', bank), ('xin', k3)], [('xo', k3)])
                    self.dma(xdst[lb * 128:(lb + 1) * 128, cg * 256:(cg + 1) * 256], xo[k3][:], [('xo', k3)], [('xdst', lb, cg)])
            self.flush()


def core_consts(j, NB):
    NQ = 4 * NB
    cst = np.zeros((128, 1408), np.float32)
    cst[:, 0:128] = np.eye(128, dtype=np.float32)
    cst[:, 128:256] = np.eye(128, dtype=np.float32)[::-1]
    s = np.arange(128)
    cst[:, 256:384] = (s[:, None] <= s[None, :]).astype(np.float32)
    for r in range(4):
        if r < j:
            m = np.ones((128, 128), np.float32)
        elif r == j:
            m = (s[:, None] <= s[None, :]).astype(np.float32)
        else:
            m = np.zeros((128, 128), np.float32)
        cst[:, 384 + r * 128:384 + (r + 1) * 128] = m
    for slot in range(4):
        r = 3 - slot
        if r > j:
            m = np.ones((128, 128), np.float32)
        elif r == j:
            sp = 127 - s
            m = (sp[None, :] >= s[:, None]).astype(np.float32)
        else:
            m = np.zeros((128, 128), np.float32)
        cst[:, 896 + slot * 128:896 + (slot + 1) * 128] = m
    cst2 = np.zeros((128, 256), np.float32)
    q = np.arange(NQ)
    gq = 4 * (q % NB) + q // NB
    if NQ <= 64:
        cst2[0:NQ, 0:NQ] = (gq[:, None] < gq[None, :]).astype(np.float32)
    own_g = 4 * np.arange(NB) + j
    cst2[0:NQ, 64:64 + NB] = (gq[:, None] < own_g[None, :]).astype(np.float32)
    for r in range(4):
        cst2[:, 200 + r] = -30000.0 if r > j else 0.0
    for lb in range(NB):
        for k in range(2):
            if j >= 1:
                r_, lb_ = j - 1, lb
            else:
                r_, lb_ = 3, lb - 1
            if lb_ < 0:
                continue
            cst2[r_ * NB * 2 + lb_ * 2 + k, 128 + lb * 2 + k] = 1.0
    return cst, cst2


def make_in_maps(inputs, L, NB, ncores=8):
    x = np.asarray(inputs['x'], np.float32)
    f = lambda k: np.ascontiguousarray(np.asarray(inputs[k], np.float32)[:L])
    shared = {
        'attn_norm': f('attn_norm'), 'w_in': f('w_in'), 'b_forget': f('b_forget'), 'q_norm': f('q_norm'),
        'k_norm': f('k_norm'),
        'out_norm': np.ascontiguousarray(np.concatenate([f('out_norm_fox'), f('out_norm_sb')], axis=1)),
        'w_out': f('w_out'), 'ffn_norm': f('ffn_norm'), 'w_up': f('w_up'),
        'conv_wb': np.ascontiguousarray(np.concatenate([f('conv_w'), f('conv_b')[:, None, :]], axis=1)),
        'w_down': f('w_down'),
    }
    maps = []
    for c in range(ncores):
        b, j = c // 4, c % 4
        rows = np.concatenate([np.arange((4 * lb + j) * 128, (4 * lb + j + 1) * 128) for lb in range(NB)])
        cst, cst2 = core_consts(j, NB)
        m = dict(shared)
        m['x'] = np.ascontiguousarray(x[b][rows])
        m['cst'] = cst
        m['cst2'] = cst2
        maps.append(m)
    return maps


def gather_out(results, B, S, NB, ncores=8):
    out = np.zeros((B, S, D), np.float32)
    for c in range(ncores):
        b, j = c // 4, c % 4
        yc = np.asarray(results[c]['y'])
        for lb in range(NB):
            g = 4 * lb + j
            out[b, g * 128:(g + 1) * 128] = yc[lb * 128:(lb + 1) * 128]
    return out


_CACHE = {}


def kernel(**inputs):
    L = int(np.asarray(inputs['w_in']).shape[0])
    Bn, S, _ = np.asarray(inputs['x']).shape
    NB = S // 512
    key = (L, NB)
    if key not in _CACHE:
        _CACHE[key] = Builder(L, NB).build()
    nc = _CACHE[key]
    in_maps = make_in_maps(inputs, L, NB)
    res = run_bass_kernel_spmd(nc, in_maps, core_ids=list(range(8)))
    return gather_out(res.results, Bn, S, NB)
```

```python
import numpy as np
from contextlib import ExitStack
import concourse.bass as bass
import concourse.mybir as mybir
from concourse.bass_utils import run_bass_kernel_spmd

F32 = mybir.dt.float32
BF16 = mybir.dt.bfloat16
AF = mybir.ActivationFunctionType
ALU = mybir.AluOpType

D = 2048
HD = 128
NH = 8
DFF = 5632
INC = 6152
KC = D // 128
FC = DFF // 128
EPS = 1e-6
SCALE = HD ** -0.5

ENGS = ['pe', 'act', 'dve', 'pool', 'sp']
EPOCH = 20000
DMA_POOL = 12
SAME_ENGINE_SYNC = True


class Tok:
    __slots__ = ('sem', 'val', 'eng')

    def __init__(self, sem, val, eng):
        self.sem = sem
        self.val = val
        self.eng = eng


class Sched:
    def __init__(self, nc, sems):
        self.nc = nc
        self.sem_iter = iter(sems)
        self.ops = {e: [] for e in ENGS}
        self.prog = {e: [] for e in ENGS}
        self.count = {e: 0 for e in ENGS}
        self.known = {e: {} for e in ENGS}
        self.res = {}
        self.dma_sems = {}
        self.dma_cnt = {}
        self.dma_n = {}
        self.cc_sem = None
        self.cc_cnt = 0

    def _new_sem(self):
        return next(self.sem_iter)

    def _need(self, eng, tok, waits):
        if tok is None:
            return
        if tok.eng == 'pe' and eng == 'pe':
            return
        if (not SAME_ENGINE_SYNC) and tok.eng == eng:
            return
        k = self.known[eng]
        sid = id(tok.sem)
        if k.get(sid, 0) >= tok.val:
            return
        k[sid] = tok.val
        for i, (s, v) in enumerate(waits):
            if s is tok.sem:
                waits[i] = (s, max(v, tok.val))
                return
        waits.append((tok.sem, tok.val))

    def _deps(self, eng, reads, writes, waits):
        for r in reads:
            st = self.res.get(r)
            if st is not None:
                self._need(eng, st[0], waits)
        for w in writes:
            st = self.res.get(w)
            if st is not None:
                self._need(eng, st[0], waits)
                for t in st[1].values():
                    self._need(eng, t, waits)

    def _commit(self, tok, reads, writes):
        for r in reads:
            st = self.res.get(r)
            if st is None:
                st = [None, {}]
                self.res[r] = st
            st[1][id(tok.sem)] = tok
        for w in writes:
            self.res[w] = [tok, {}]

    def op(self, eng, fn, reads=(), writes=()):
        waits = []
        self._deps(eng, reads, writes, waits)
        n = self.count[eng]
        ep, k = divmod(n, EPOCH)
        while len(self.prog[eng]) <= ep:
            self.prog[eng].append(self._new_sem())
        sem = self.prog[eng][ep]
        tok = Tok(sem, k + 1, eng)
        self.count[eng] = n + 1
        self.ops[eng].append((waits, fn, sem, 1))
        self._commit(tok, reads, writes)
        return tok

    def dma(self, q, fn, reads=(), writes=(), ndma=1):
        if q not in self.dma_sems:
            self.dma_sems[q] = [self._new_sem() for _ in range(DMA_POOL)]
            self.dma_cnt[q] = [0] * DMA_POOL
            self.dma_n[q] = 0
        waits = []
        self._deps(q, reads, writes, waits)
        slot = self.dma_n[q] % DMA_POOL
        self.dma_n[q] += 1
        sem = self.dma_sems[q][slot]
        prev = self.dma_cnt[q][slot]
        if prev > 0:
            self._need(q, Tok(sem, prev, 'dma'), waits)
        newv = prev + 16 * ndma
        self.dma_cnt[q][slot] = newv
        tok = Tok(sem, newv, 'dma')
        self.ops[q].append((waits, fn, sem, -ndma))
        self._commit(tok, reads, writes)
        return tok

    def collective(self, fn, reads, writes):
        if self.cc_sem is None:
            self.cc_sem = self._new_sem()
        waits = []
        self._deps('pool', reads, writes, waits)
        self.cc_cnt += 1
        tok = Tok(self.cc_sem, self.cc_cnt, 'cc')
        self.ops['pool'].append((waits, fn, self.cc_sem, 1))
        self._commit(tok, reads, writes)
        return tok

    def drain(self):
        toks = []
        for st in self.res.values():
            if st[0] is not None:
                toks.append(st[0])
            toks.extend(st[1].values())
        for e in ENGS:
            waits = []
            for t in toks:
                if t.eng == 'pe' and e == 'pe':
                    continue
                k = self.known[e]
                sid = id(t.sem)
                if k.get(sid, 0) >= t.val:
                    continue
                k[sid] = t.val
                for i, (s, v) in enumerate(waits):
                    if s is t.sem:
                        waits[i] = (s, max(v, t.val))
                        break
                else:
                    waits.append((t.sem, t.val))
            if waits:
                self.ops[e].append((waits, None, None, 0))
        self.res = {}

    def emit(self, block):
        names = {'pe': 'tensor', 'act': 'scalar', 'dve': 'vector', 'pool': 'gpsimd', 'sp': 'sync'}

        def make(e):
            lst = self.ops[e]

            def body(eng):
                for waits, fn, sem, inc in lst:
                    for s, v in waits:
                        eng.wait_ge(s, v)
                    if fn is None:
                        continue
                    if inc > 0:
                        fn(eng).then_inc(sem, 1)
                    else:
                        cnt = [0]

                        def incf(ins, sem=sem, cnt=cnt):
                            cnt[0] += 1
                            return ins.then_inc(sem, 16)
                        fn(eng, incf)
                        assert cnt[0] == -inc, (cnt[0], inc)
            return body

        for e in ENGS:
            if self.ops[e]:
                getattr(block, names[e])(make(e))
        self.ops = {e: [] for e in ENGS}


class Builder:
    def __init__(self, L, NB, dbg=(), ncores=8):
        self.ncores = ncores
        self.rg = [[0, 1, 2, 3], [4, 5, 6, 7]] if ncores == 8 else [[0, 1, 2, 3]]
        self.L = L
        self.NB = NB
        self.NT = NB * 128
        self.dbg = set(dbg)
        NT = self.NT
        self.KTF_OFF = 0
        self.VF_OFF = 8 * NT
        self.KTS_OFF = self.VF_OFF + 8 * NB * 129
        self.VS_OFF = self.KTS_OFF + 8 * NT
        self.CW = self.VS_OFF + 8 * NB * 128
        self.NQ = 4 * NB

    def mm(self, out, lhsT, rhs, start, stop, r, w, tr=False):
        if tr:
            self.S.op('pe', lambda e: e.matmul(out, lhsT=lhsT, rhs=rhs, start=start, stop=stop, is_transpose=True), r, w)
        else:
            self.S.op('pe', lambda e: e.matmul(out, lhsT=lhsT, rhs=rhs, start=start, stop=stop), r, w)

    def act(self, out, in_, func, r, w, bias=None, scale=None, accum=None):
        kw = {}
        if bias is not None:
            kw['bias'] = bias
        if scale is not None:
            kw['scale'] = scale
        if accum is not None:
            kw['accum_out'] = accum
        self.S.op('act', lambda e: e.activation(out=out, in_=in_, func=func, **kw), r, w)

    def ts(self, eng, out, in0, s1, s2, op0, op1, r, w):
        if op1 is None:
            self.S.op(eng, lambda e: e.tensor_scalar(out=out, in0=in0, scalar1=s1, scalar2=None, op0=op0), r, w)
        else:
            self.S.op(eng, lambda e: e.tensor_scalar(out=out, in0=in0, scalar1=s1, scalar2=s2, op0=op0, op1=op1), r, w)

    def tt(self, eng, out, in0, in1, op, r, w):
        self.S.op(eng, lambda e: e.tensor_tensor(out=out, in0=in0, in1=in1, op=op), r, w)

    def stt(self, out, in0, scalar, in1, op0, op1, r, w):
        self.S.op('dve', lambda e: e.scalar_tensor_tensor(out=out, in0=in0, scalar=scalar, in1=in1, op0=op0, op1=op1), r, w)

    def cp(self, eng, out, in_, r, w):
        if eng == 'act':
            self.S.op('act', lambda e: e.copy(out=out, in_=in_), r, w)
        else:
            self.S.op(eng, lambda e: e.tensor_copy(out=out, in_=in_), r, w)

    def memset(self, eng, ap, val, w):
        self.S.op(eng, lambda e: e.memset(ap, val), (), w)

    def dma(self, out, in_, r, w, q='sp'):
        self.S.dma(q, lambda e, inc: inc(e.dma_start(out=out, in_=in_)), r, w)

    def rstd_from_ssq(self, ssq, tmp, n, r, w):
        self.ts('dve', tmp, ssq, 1.0 / n, EPS, ALU.mult, ALU.add, r, w)
        self.act(tmp, tmp, AF.Sqrt, w, w)
        self.S.op('dve', lambda e: e.reciprocal(out=tmp, in_=tmp), w, w)

    def build(self):
        L, NB, NT = self.L, self.NB, self.NT
        nc = bass.Bass("TRN2", target_bir_lowering=False)
        self.nc = nc
        dt_in = lambda n, s, dt=F32: nc.dram_tensor(n, s, dt, kind="ExternalInput").ap()
        I = {}
        I['x'] = dt_in("x", [NT, D])
        I['attn_norm'] = dt_in("attn_norm", [L, D])
        I['w_in'] = dt_in("w_in", [L, D, INC])
        I['b_forget'] = dt_in("b_forget", [L, NH])
        I['q_norm'] = dt_in("q_norm", [L, HD])
        I['k_norm'] = dt_in("k_norm", [L, HD])
        I['out_norm'] = dt_in("out_norm", [L, D])
        I['w_out'] = dt_in("w_out", [L, D, D])
        I['ffn_norm'] = dt_in("ffn_norm", [L, D])
        I['w_up'] = dt_in("w_up", [L, D, 2 * DFF])
        I['conv_wb'] = dt_in("conv_wb", [L, 4, 2 * DFF])
        I['w_down'] = dt_in("w_down", [L, DFF, D])
        I['cst'] = dt_in("cst", [128, 1408])
        I['cst2'] = dt_in("cst2", [128, 256])
        self.I = I
        y = nc.dram_tensor("y", [NT, D], F32, kind="ExternalOutput").ap()
        self.y = y
        dti = lambda n, s, dt: nc.dram_tensor(n, s, dt).ap()
        self.xa_d = dti("xa_d", [NT, D], F32)
        self.xres_d = dti("xres_d", [NT, D], F32)
        self.qT_d = dti("qT_d", [16, 128, NT], BF16)
        self.h2T_d = dti("h2T_d", [128, KC, NB, 130], BF16)
        self.g_in = {}
        self.g_out = {}
        for p in range(2):
            for hp in range(4):
                for nm, w in (('kf', 2 * NT), ('ks', 2 * NT), ('vs', 2 * NB * 128)):
                    self.g_in[(nm, p, hp)] = dti(f"gi_{nm}{p}_{hp}", [128, w], BF16)
                    self.g_out[(nm, p, hp)] = dti(f"go_{nm}{p}_{hp}", [512, w], BF16)
            for h in range(8):
                self.g_in[('vf', p, h)] = dti(f"gi_vf{p}_{h}", [128, NB * 129], BF16)
                self.g_out[('vf', p, h)] = dti(f"go_vf{p}_{h}", [512, NB * 129], BF16)
        self.gin2 = [dti(f"gin2_{p}", [129, NB * 8], F32) for p in range(2)]
        self.gout2 = [dti(f"gout2_{p}", [4 * 129, NB * 8], F32) for p in range(2)]
        self.gin3 = [dti(f"gin3_{p}", [NB * 2, D], BF16) for p in range(2)]
        self.gout3 = [dti(f"gout3_{p}", [4 * NB * 2, D], BF16) for p in range(2)]
        self.dbg_out = {}
        if 'merged' in self.dbg:
            self.dbg_out['merged'] = nc.dram_tensor("dbg_merged", [128, NB, D], BF16, kind="ExternalOutput").ap()
        if 'xa' in self.dbg:
            self.dbg_out['xa'] = nc.dram_tensor("dbg_xa", [NT, D], F32, kind="ExternalOutput").ap()

        with ExitStack() as es:
            sems = [es.enter_context(nc.semaphore(f"s{i}")) for i in range(64)]
            self.S = Sched(nc, sems)
            self.ps = [es.enter_context(nc.psum_tensor(f"ps{k}", [128, 512], F32)) for k in range(8)]
            self.psb = [p[:].bitcast(BF16) for p in self.ps]
            sb = lambda n, s, dt: es.enter_context(nc.sbuf_tensor("c_" + n, s, dt))
            self.cst_f = sb("cst_f", [128, 1408], F32)
            self.identb = sb("identb", [128, 128], BF16)
            self.Jb = sb("Jb", [128, 128], BF16)
            self.nidentb = sb("nidentb", [128, 128], BF16)
            self.onesb = sb("onesb", [128, 128], BF16)
            self.mfox = sb("mfox", [128, 512], BF16)
            self.cst2 = sb("cst2", [128, 256], F32)
            self.selb = sb("selb", [128, 2 * NB], BF16)
            self.zeros = sb("zeros", [128, 512], F32)
            self.ones_f = sb("ones_f", [128, 128], F32)
            self.tri = self.cst_f[:, 256:384]
            self.nmsb = self.cst_f[:, 896:1408]
            self.ident_f = self.cst_f[:, 0:128]
            self.phase_consts()
            for l in range(L):
                self.layer(l)
        return nc

    def run_block(self, fn):
        import os
        self._nph = getattr(self, '_nph', 0) + 1
        if self._nph > int(os.environ.get('MAXPH', '1000')):
            return
        fn()
        self.flush()

    def flush(self):
        if not any(self.S.ops[e] for e in ENGS):
            return
        self.S.drain()
        with self.nc.Block() as block:
            self.S.emit(block)

    def phase_consts(self):
        def f():
            I = self.I
            self.dma(self.cst_f[:, 0:384], I['cst'][:, 0:384], (), ['cst_f'])
            self.dma(self.cst_f[:, 384:1408], I['cst'][:, 384:1408], (), ['cst_f2'])
            self.dma(self.cst2[:], I['cst2'], (), ['cst2'])
            self.cp('dve', self.identb[:], self.cst_f[:, 0:128], ['cst_f'], ['identb'])
            self.cp('dve', self.Jb[:], self.cst_f[:, 128:256], ['cst_f'], ['Jb'])
            self.ts('dve', self.nidentb[:], self.cst_f[:, 0:128], -1.0, None, ALU.mult, None, ['cst_f'], ['nidentb'])
            self.cp('dve', self.mfox[:], self.cst_f[:, 384:896], ['cst_f2'], ['mfox'])
            self.cp('dve', self.selb[:], self.cst2[:, 128:128 + 2 * self.NB], ['cst2'], ['selb'])
            self.memset('pool', self.onesb[:], 1.0, ['onesb'])
            self.memset('pool', self.zeros[:], 0.0, ['zeros'])
            self.memset('pool', self.ones_f[:], 1.0, ['ones_f'])
        self.run_block(f)

    def layer(self, l):
        nc = self.nc
        NB, NT = self.NB, self.NT
        par = l % 2
        xsrc = self.I['x'] if l == 0 else self.xres_d
        xdst = self.y if l == self.L - 1 else self.xres_d
        with ExitStack() as es1:
            hT = es1.enter_context(nc.sbuf_tensor(f"hT{l}", [128, KC, NT], BF16))
            self.run_block(lambda: self.phase_norm_T(l, xsrc, self.I['attn_norm'][l], hT, 'A1', halo=None))
            self.run_block(lambda: self.phase_A2(l, hT))
        def g():
            S = self.S
            pass
        self.run_block(g)
        with ExitStack() as es2:
            merged = es2.enter_context(nc.sbuf_tensor(f"merged{l}", [128, NB, D], BF16))
            self.run_block(lambda: self.phase_B(l, merged))
            mT = es2.enter_context(nc.sbuf_tensor(f"mT{l}", [128, KC, NT], BF16))
            self.run_block(lambda: self.phase_norm_T(l, None, self.I['out_norm'][l], mT, 'C1', halo=None, merged=merged))
            self.run_block(lambda: self.phase_C2(l, mT, xsrc))
        with ExitStack() as es3:
            halo_s = es3.enter_context(nc.sbuf_tensor(f"halo{l}", [128, KC, NB, 2], BF16))
            self.run_block(lambda: self.phase_norm_T(l, self.xa_d, self.I['ffn_norm'][l], None, 'C3', halo=par))
            def g3():
                rg = self.rg
                gi, go = self.gin3[par], self.gout3[par]
                self.S.collective(lambda e: e.collective_compute("AllGather", ALU.bypass, replica_groups=rg, ins=[gi], outs=[go]), [], ['gout3'])
            self.run_block(g3)
            self.run_block(lambda: self.phase_halo(l, halo_s))
            self.run_block(lambda: self.phase_FFN(l, halo_s, xdst))

    def phase_norm_T(self, l, xsrc, gvec, dstT, tag, halo, merged=None, es_ext=None):
        nc = self.nc
        NB = self.NB
        with ExitStack() as es_own:
            es = es_own if es_ext is None else es_ext
            sb = lambda n, s, dt: es.enter_context(nc.sbuf_tensor(f"{tag}{l}_{n}", s, dt))
            gB = sb("gB", [128, D], F32)
            gBb = sb("gBb", [128, D], BF16)
            hx = [sb(f"hx{k}", [128, D], BF16) for k in range(2)]
            hb = [sb(f"hb{k}", [128, D], BF16) for k in range(2)]
            junk = sb("junk", [128, D], BF16)
            ssq = sb("ssq", [128, 2 * NB], F32)
            rstd = sb("rstd", [128, 2 * NB], F32)
            xb = None
            if merged is None:
                xb = [sb(f"xb{k}", [128, D], F32) for k in range(2)]
            hts = None
            if tag == 'C3':
                hts = [sb(f"hts{k}", [128, KC, 4, 130], BF16) for k in range(2)]
            self.dma(gB[:], gvec.partition_broadcast(128), (), ['gB'])
            self.cp('pool', gBb[:], gB[:], ['gB'], ['gBb'])
            for lb in range(NB):
                b2 = lb % 2
                if merged is None:
                    self.dma(xb[b2][:], xsrc[lb * 128:(lb + 1) * 128, :], [], [('xb', b2)])
                    self.act(junk[:], xb[b2][:], AF.Square, [('xb', b2)], ['junk', ('ssq', lb)], accum=ssq[:, lb:lb + 1])
                    self.rstd_from_ssq(ssq[:, lb:lb + 1], rstd[:, lb:lb + 1], D, [('ssq', lb)], [('rstd', lb)])
                    self.act(hx[b2][:], xb[b2][:], AF.Identity, [('xb', b2), ('rstd', lb)], [('hx', b2)], scale=rstd[:, lb:lb + 1])
                    self.tt('dve', hb[b2][:], hx[b2][:], gBb[:], ALU.mult, [('hx', b2), 'gBb'], [('hb', b2)])
                else:
                    for g in range(2):
                        self.act(junk[:, 0:1024], merged[:, lb, g * 1024:(g + 1) * 1024], AF.Square, ['merged'],
                                 ['junk', ('ssq', lb)], accum=ssq[:, 2 * lb + g:2 * lb + g + 1])
                    self.rstd_from_ssq(ssq[:, 2 * lb:2 * lb + 2], rstd[:, 2 * lb:2 * lb + 2], 1024, [('ssq', lb)], [('rstd', lb)])
                    for g in range(2):
                        self.act(hx[b2][:, g * 1024:(g + 1) * 1024], merged[:, lb, g * 1024:(g + 1) * 1024], AF.Identity,
                                 ['merged', ('rstd', lb)], [('hx', b2, g)], scale=rstd[:, 2 * lb + g:2 * lb + g + 1])
                    self.tt('dve', hb[b2][:], hx[b2][:], gBb[:], ALU.mult, [('hx', b2, 0), ('hx', b2, 1), 'gBb'], [('hb', b2)])
                if tag == 'C3':
                    self.dma(self.gin3[halo][2 * lb:2 * lb + 2, :], hb[b2][126:128, :], [('hb', b2)], [('gin3', lb)])
                for half in range(2):
                    bank = 2 * (lb % 2) + half
                    for k8 in range(8):
                        kc = half * 8 + k8
                        self.mm(self.psb[bank][:, k8 * 128:(k8 + 1) * 128], hb[b2][:, kc * 128:(kc + 1) * 128], self.identb[:],
                                True, True, [('hb', b2), 'identb'], [('ps', bank)], tr=True)
                    src = self.psb[bank][:, 0:1024].rearrange("p (a b) -> p a b", a=8)
                    eng = 'act' if half == 0 else 'dve'
                    if tag == 'C3':
                        s4 = (lb // 4) % 2
                        self.cp(eng, hts[s4][:, half * 8:half * 8 + 8, lb % 4, 2:130], src, [('ps', bank)], [('hts', s4, half, lb % 4)])
                    else:
                        self.cp(eng, dstT[:, half * 8:half * 8 + 8, lb * 128:(lb + 1) * 128], src, [('ps', bank)], [('dstT', lb, half)])
                if tag == 'C3' and (lb % 4 == 3 or lb == NB - 1):
                    s4 = (lb // 4) % 2
                    nb4 = lb % 4 + 1
                    lb0 = lb - (nb4 - 1)
                    self.dma(self.h2T_d[:, :, lb0:lb0 + nb4, :], hts[s4][:, :, 0:nb4, :],
                             [('hts', s4, h, q) for h in range(2) for q in range(nb4)], [('h2T_d', lb0)])
            if es_ext is None:
                self.flush()

    def wload(self, wsrc, rows_kc, c0, ncols, wst, wb, slot, rname, row0=0, stslot=None):
        if stslot is None:
            stslot = slot
        src = wsrc[row0:row0 + rows_kc * 128, c0:c0 + ncols].rearrange("(kc p) c -> p kc c", p=128)
        self.dma(wst[:, 0:rows_kc, 0:ncols], src, [], [(rname + 'st', stslot)])
        self.cp('pool', wb[:, 0:rows_kc, 0:ncols], wst[:, 0:rows_kc, 0:ncols], [(rname + 'st', stslot)], [(rname, slot)])

    def gather(self, key, reads):
        rg = self.rg
        a, b = self.g_in[key], self.g_out[key]
        self.S.collective(lambda e: e.collective_compute("AllGather", ALU.bypass, replica_groups=rg, ins=[a], outs=[b]), reads, [('gout', key)])

    def phase_A2(self, l, hT, mid=None):
        nc = self.nc
        NB, NT = self.NB, self.NT
        I = self.I
        par = l % 2
        gin2 = self.gin2[par]
        GI = self.g_in
        W = I['w_in'][l]
        NTT = (NT + 511) // 512
        with ExitStack() as es:
            sb = lambda n, s, dt: es.enter_context(nc.sbuf_tensor(f"A2{l}_{n}", s, dt))
            wst = [sb(f"wst{k}", [128, KC, 256], F32) for k in range(2)]
            wb = [sb(f"wb{k}", [128, KC, 256], BF16) for k in range(2)]
            ost = [sb(f"ost{k}", [128, NT], BF16) for k in range(2)]
            vst = [sb(f"vst{k}", [128, 2, NB, 129], BF16) for k in range(2)]
            sqb = [sb(f"sqb{k}", [128, 512], BF16) for k in range(2)]
            rs = [sb(f"rs{k}", [128, 512], F32) for k in range(2)]
            tm = [sb(f"tm{k}", [128, 256], BF16) for k in range(2)]
            gq = sb("gq", [128, 2], F32)
            wfst = sb("wfst", [128, KC, 8], F32)
            wfb = sb("wfb", [128, KC, 8], BF16)
            bfB = sb("bfB", [128, 8], F32)
            lg = sb("lg", [128, NB, 8], F32)
            cwst = sb("cwst", [128, NB * 8], F32)
            for k in range(2):
                self.memset('pool', vst[k][:, :, :, 128:129], 1.0, [('vst', k)])
            self.dma(gq[:, 0:1], I['q_norm'][l].rearrange("(p o) -> p o", o=1), [], ['gq0'])
            self.dma(gq[:, 1:2], I['k_norm'][l].rearrange("(p o) -> p o", o=1), [], ['gq1'])
            self.dma(wfst[:], W[:, 3072:3080].rearrange("(kc p) c -> p kc c", p=128), [], ['wfst'])
            self.dma(bfB[:], I['b_forget'][l].partition_broadcast(128), [], ['bfB'])
            self.cp('pool', wfb[:], wfst[:], ['wfst'], ['wfb'])

            groups = []
            for gI in range(4):
                groups.append((gI * 256, 'fq', 2 * gI))
            for gI in range(4):
                groups.append((1024 + gI * 256, 'fk', 2 * gI))
            for gI in range(4):
                groups.append((2048 + gI * 256, 'fv', 2 * gI))
            for gI in range(4):
                groups.append((3080 + gI * 256, 'sq', 2 * gI))
            for gI in range(4):
                groups.append((4104 + gI * 256, 'sk', 2 * gI))
            for gI in range(4):
                groups.append((5128 + gI * 256, 'sv', 2 * gI))
            self.wload(W, KC, groups[0][0], 256, wst[0], wb[0], 0, 'w')
            if mid is not None:
                mid(es)
            pend = [None]
            ostn = 0
            bankrr = 0
            for n, (c0, kind, h0) in enumerate(groups):
                slot = n % 2
                if n + 1 < len(groups):
                    self.wload(W, KC, groups[n + 1][0], 256, wst[(n + 1) % 2], wb[(n + 1) % 2], (n + 1) % 2, 'w')
                if kind in ('fq', 'fk', 'sq'):
                    for hh in range(2):
                        head = h0 + hh
                        o = ost[ostn % 2]
                        orn = ('ost', ostn % 2)
                        ostn += 1
                        for tt in range(NTT):
                            t0 = tt * 512
                            tn = min(512, NT - t0)
                            bank = bankrr % 4
                            bankrr += 1
                            for kc in range(KC):
                                self.mm(self.ps[bank][:, 0:tn], wb[slot][:, kc, hh * 128:(hh + 1) * 128], hT[:, kc, t0:t0 + tn],
                                        kc == 0, kc == KC - 1, [('w', slot)] + [('dstT', b_, kc // 8) for b_ in range(t0 // 128, (t0 + tn) // 128)],
                                        [('ps', bank)])
                            if kind == 'sq':
                                self.cp('act', o[:, t0:t0 + tn], self.ps[bank][:, 0:tn], [('ps', bank)], [orn])
                            else:
                                k2 = tt % 2
                                gcol = gq[:, 0:1] if kind == 'fq' else gq[:, 1:2]
                                self.act(sqb[k2][:, 0:tn], self.ps[bank][:, 0:tn], AF.Square, [('ps', bank)], [('sqb', k2)])

                                def rest(k2=k2, tn=tn, bank=bank, o=o, t0=t0, gcol=gcol, orn=orn):
                                    sb_ = 4 + k2
                                    self.mm(self.ps[sb_][:, 0:tn], self.onesb[:], sqb[k2][:, 0:tn], True, True,
                                            [('sqb', k2), 'onesb'], [('ps', sb_)])
                                    self.ts('dve', rs[k2][:, 0:tn], self.ps[sb_][:, 0:tn], 1.0 / HD, EPS, ALU.mult, ALU.add,
                                            [('ps', sb_)], [('rs', k2)])
                                    self.act(rs[k2][:, 0:tn], rs[k2][:, 0:tn], AF.Sqrt, [('rs', k2)], [('rs', k2)])
                                    self.S.op('dve', (lambda a: (lambda e: e.reciprocal(out=a, in_=a)))(rs[k2][:, 0:tn]), [('rs', k2)], [('rs', k2)])
                                    self.stt(o[:, t0:t0 + tn], self.ps[bank][:, 0:tn], gcol, rs[k2][:, 0:tn], ALU.mult, ALU.mult,
                                             [('ps', bank), ('rs', k2), 'gq0', 'gq1'], [orn])
                                if pend[0] is not None:
                                    pend[0]()
                                pend[0] = rest
                        if pend[0] is not None:
                            pend[0]()
                            pend[0] = None
                        if kind == 'fq':
                            self.dma(self.qT_d[head], o[:], [orn], [('qT_d', head)])
                        elif kind == 'sq':
                            self.dma(self.qT_d[8 + head], o[:], [orn], [('qT_d', 8 + head)])
                        else:
                            self.dma(GI[('kf', par, head // 2)][:, (head % 2) * NT:(head % 2 + 1) * NT], o[:], [orn], [('gin', 'kf', head)])
                            if hh == 1:
                                self.gather(('kf', par, head // 2), [('gin', 'kf', head - 1), ('gin', 'kf', head)])
                elif kind == 'fv':
                    v = vst[(n // 1) % 2]
                    vrn = ('vst', n % 2)
                    for lb in range(NB):
                        bank = bankrr % 4
                        bankrr += 1
                        for kc in range(KC):
                            self.mm(self.ps[bank][:, 0:256], hT[:, kc, lb * 128:(lb + 1) * 128], wb[slot][:, kc, :],
                                    kc == 0, kc == KC - 1, [('w', slot), ('dstT', lb, kc // 8)], [('ps', bank)])
                        self.cp('act', v[:, :, lb, 0:128], self.ps[bank][:, 0:256].rearrange("p (a b) -> p a b", a=2),
                                [('ps', bank)], [vrn])
                    for hh in range(2):
                        head = h0 + hh
                        self.dma(GI[('vf', par, head)][:, :], v[:, hh].rearrange("p a b -> p (a b)"), [vrn], [('gin', 'vf', head)])
                        self.gather(('vf', par, head), [('gin', 'vf', head)])
                elif kind == 'sk':
                    o2 = [ost[ostn % 2], ost[(ostn + 1) % 2]]
                    orn2 = [('ost', ostn % 2), ('ost', (ostn + 1) % 2)]
                    ostn += 2
                    for lb in range(NB):
                        bank = bankrr % 4
                        bankrr += 1
                        for kc in range(KC):
                            self.mm(self.ps[bank][:, 0:256], hT[:, kc, lb * 128:(lb + 1) * 128], wb[slot][:, kc, :],
                                    kc == 0, kc == KC - 1, [('w', slot), ('dstT', lb, kc // 8)], [('ps', bank)])
                        k2 = lb % 2
                        self.cp('act', tm[k2][:], self.ps[bank][:, 0:256], [('ps', bank)], [('tm', k2)])

                        def rest(k2=k2, lb=lb, o2=o2, orn2=orn2):
                            jb = 4 + k2
                            for hh in range(2):
                                self.mm(self.ps[jb][:, hh * 128:(hh + 1) * 128], tm[k2][:, hh * 128:(hh + 1) * 128], self.Jb[:], True, True,
                                        [('tm', k2), 'Jb'], [('ps', jb)])
                            for hh in range(2):
                                self.cp('dve', o2[hh][:, lb * 128:(lb + 1) * 128], self.ps[jb][:, hh * 128:(hh + 1) * 128], [('ps', jb)], [orn2[hh]])
                        if pend[0] is not None:
                            pend[0]()
                        pend[0] = rest
                    if pend[0] is not None:
                        pend[0]()
                        pend[0] = None
                    for hh in range(2):
                        head = h0 + hh
                        self.dma(GI[('ks', par, head // 2)][:, (head % 2) * NT:(head % 2 + 1) * NT], o2[hh][:], [orn2[hh]], [('gin', 'ks', head)])
                    self.gather(('ks', par, h0 // 2), [('gin', 'ks', h0), ('gin', 'ks', h0 + 1)])
                elif kind == 'sv':
                    v = vst[n % 2]
                    vrn = ('vst', n % 2)
                    for lb in range(NB):
                        bank = bankrr % 4
                        bankrr += 1
                        for kc in range(KC):
                            self.mm(self.ps[bank][:, 0:256], hT[:, kc, lb * 128:(lb + 1) * 128], wb[slot][:, kc, :],
                                    kc == 0, kc == KC - 1, [('w', slot), ('dstT', lb, kc // 8)], [('ps', bank)])
                        k2 = lb % 2
                        self.cp('act', tm[k2][:], self.ps[bank][:, 0:256], [('ps', bank)], [('tm', k2)])

                        def rest(k2=k2, lb=lb, v=v, vrn=vrn):
                            jb = 4 + k2
                            self.mm(self.ps[jb][:, 0:256], self.Jb[:], tm[k2][:], True, True, [('tm', k2), 'Jb'], [('ps', jb)])
                            self.cp('dve', v[:, :, lb, 0:128], self.ps[jb][:, 0:256].rearrange("p (a b) -> p a b", a=2), [('ps', jb)], [vrn])
                        if pend[0] is not None:
                            pend[0]()
                        pend[0] = rest
                    if pend[0] is not None:
                        pend[0]()
                        pend[0] = None
                    for hh in range(2):
                        head = h0 + hh
                        c = (head % 2) * NB * 128
                        self.dma(GI[('vs', par, head // 2)][:, c:c + NB * 128].rearrange("p (a b) -> p a b", a=NB), v[:, hh, :, 0:128], [vrn], [('gin', 'vs', head)])
                    self.gather(('vs', par, h0 // 2), [('gin', 'vs', h0), ('gin', 'vs', h0 + 1)])
            for lb in range(NB):
                bank = 6 + lb % 2
                for kc in range(KC):
                    self.mm(self.ps[bank][:, 0:8], hT[:, kc, lb * 128:(lb + 1) * 128], wfb[:, kc, :], kc == 0, kc == KC - 1,
                            ['wfb', ('dstT', lb, kc // 8)], [('ps', bank)])
                self.tt('dve', lg[:, lb, :], self.ps[bank][:, 0:8], bfB[:], ALU.add, [('ps', bank), 'bfB'], [('lg', lb)])
                self.act(lg[:, lb, :], lg[:, lb, :], AF.Exp, [('lg', lb)], [('lg', lb)], scale=-1.0)
                self.act(lg[:, lb, :], lg[:, lb, :], AF.Ln, [('lg', lb)], [('lg', lb)], bias=1.0)
                self.mm(self.ps[bank][:, 16:24], self.tri, lg[:, lb, :], True, True, [('lg', lb), 'cst_f'], [('ps', bank)])
                self.cp('dve', cwst[:, lb * 8:(lb + 1) * 8], self.ps[bank][:, 16:24], [('ps', bank)], ['cwst'])
            self.dma(gin2[0:128, :], cwst[:], ['cwst'], ['gin2a'])
            self.dma(gin2[128:129, :], cwst[127:128, :], ['cwst'], ['gin2b'])
            rg = self.rg
            go2 = self.gout2[par]
            self.S.collective(lambda e: e.collective_compute("AllGather", ALU.bypass, replica_groups=rg, ins=[gin2], outs=[go2]), ['gin2a', 'gin2b'], ['gout2'])
            self.flush()

    def phase_B(self, l, merged):
        nc = self.nc
        NB, NT, NQ = self.NB, self.NT, self.NQ
        par = l % 2
        gout2 = self.gout2[par]
        GO = self.g_out
        with ExitStack() as es:
            sb = lambda n, s, dt: es.enter_context(nc.sbuf_tensor(f"B{l}_{n}", s, dt))
            KT = [sb(f"KT{k}", [128, 4, NT], BF16) for k in range(2)]
            VV = [sb(f"VV{k}", [128, 4, NB * 129], BF16) for k in range(2)]
            QT = [sb(f"QT{k}", [128, NT], BF16) for k in range(2)]
            cwS = sb("cwS", [128, 4, NB * 8], F32)
            bsT = sb("bsT", [64, 8], F32)
            bsbc = sb("bsbc", [64, 128], F32)
            Dt = sb("Dt", [128, NQ], F32)
            own = sb("own", [128, NB], F32)
            bias4 = [sb(f"bias4_{k}", [128, 4, NQ], F32) for k in range(2)]
            NPT = 6
            pT = [sb(f"pT{k}", [128, 128], BF16) for k in range(NPT)]
            rden = sb("rden", [128, 8], F32)
            NA = 4
            a_ = [sb(f"a{k}", [128, 512], F32) for k in range(NA)]
            buf = [sb(f"buf{k}", [128, 514], BF16) for k in range(NA)]
            NAB = 3
            AT = [sb(f"AT{k}", [128, 512], BF16) for k in range(NAB)]
            Lall = self.cst2[0:NQ, 0:NQ] if NQ <= 64 else None
            Lown = self.cst2[0:NQ, 64:64 + NB]

            self.dma(cwS[:], gout2.rearrange("(r q) c -> q r c", q=129)[0:128], [], ['cwS'])
            for r in range(4):
                self.dma(bsT[r * NB:(r + 1) * NB, :], gout2[r * 129 + 128, :].rearrange("(a b) -> a b", b=8), [], [('bsT', r)])
            bsT_r = [('bsT', r) for r in range(4)]

            def load_head(kind, h, k):
                if kind == 'f':
                    c = (h % 2) * NT
                    self.dma(KT[k][:], GO[('kf', par, h // 2)].rearrange("(r p) c -> p r c", p=128)[:, :, c:c + NT], [], [('KT', k, r) for r in range(4)])
                    self.dma(VV[k][:], GO[('vf', par, h)].rearrange("(r p) c -> p r c", p=128), [], [('VV', k, r) for r in range(4)])
                    self.dma(QT[k][:], self.qT_d[h], [], [('QT', k)])
                else:
                    c = (h % 2) * NT
                    cv = (h % 2) * NB * 128
                    gk_ = GO[('ks', par, h // 2)]
                    gv_ = GO[('vs', par, h // 2)]
                    for r in range(4):
                        self.dma(KT[k][:, 3 - r, :], gk_[r * 128:(r + 1) * 128, c:c + NT], [], [('KT', k, r)])
                        self.dma(VV[k][:, 3 - r, 0:NB * 128], gv_[r * 128:(r + 1) * 128, cv:cv + NB * 128], [], [('VV', k, r)])
                    self.dma(QT[k][:], self.qT_d[8 + h], [], [('QT', k)])

            heads = [('f', h) for h in range(NH)] + [('s', h) for h in range(NH)]
            load_head(heads[0][0], heads[0][1], 0)
            ptn = 0
            stn = 0
            for hn, (kind, h) in enumerate(heads):
                k = hn % 2
                if hn + 1 < len(heads):
                    load_head(heads[hn + 1][0], heads[hn + 1][1], (hn + 1) % 2)
                KTk, VVk, QTk = KT[k], VV[k], QT[k]
                KTr = [('KT', k, r) for r in range(4)]
                VVr = [('VV', k, r) for r in range(4)]
                if kind == 'f':
                    self.ts('dve', bsbc[0:NQ, :], self.ones_f[0:NQ, :], bsT[0:NQ, h:h + 1], None, ALU.mult, None,
                            bsT_r + ['ones_f'], ['bsbc'])
                    self.mm(self.ps[6][:, 0:NQ], bsbc[0:NQ, :], Lall, True, True, ['bsbc', 'cst2'], [('ps', 6)])
                    self.mm(self.ps[7][:, 0:NB], bsbc[0:NQ, :], Lown, True, True, ['bsbc', 'cst2'], [('ps', 7)])
                    self.tt('dve', Dt[:].rearrange("p (r a) -> p r a", r=4),
                            cwS[:].rearrange("p r (a c) -> p r a c", c=8)[:, :, :, h], self.ps[6][:, 0:NQ].rearrange("p (r a) -> p r a", r=4),
                            ALU.add, ['cwS', ('ps', 6)], ['Dt'])
                    self.cp('dve', own[:], self.ps[7][:, 0:NB], [('ps', 7)], ['own'])
                    for i0 in range(0, NB, 4):
                        i1 = min(i0 + 4, NB)
                        gk = (i0 // 4) % 2
                        for i in range(i0, i1):
                            self.ts('dve', bias4[gk][:, i - i0, :], Dt[:], own[:, i:i + 1], None, ALU.subtract, None,
                                    ['Dt', 'own'], [('bias4', gk, i - i0)])
                            dg = bias4[gk][:, i - i0, :].rearrange("p (r a) -> p r a", r=4)[:, :, i]
                            self.tt('dve', dg, dg, self.cst2[:, 200:204], ALU.add, [('bias4', gk, i - i0), 'cst2'], [('bias4', gk, i - i0)])
                        steps = [(ip, r) for ip in range(i1) for r in range(4)]

                        def qk(n):
                            ip, r = steps[n]
                            ist = max(ip, i0)
                            ncol = (i1 - ist) * 128
                            bank = n % 2
                            self.mm(self.ps[bank][:, 0:ncol], KTk[:, r, ip * 128:(ip + 1) * 128], QTk[:, ist * 128:i1 * 128], True, True,
                                    KTr + [('QT', k)], [('ps', bank)])
                        qk(0)
                        for n, (ip, r) in enumerate(steps):
                            if n + 1 < len(steps):
                                qk(n + 1)
                            ist = max(ip, i0)
                            bank = n % 2
                            q = r * NB + ip
                            for i in range(ist, i1):
                                pk = ptn % NPT
                                ptn += 1
                                self.act(pT[pk][:], self.ps[bank][:, (i - ist) * 128:(i - ist + 1) * 128], AF.Exp,
                                         [('ps', bank), ('bias4', gk, i - i0)], [('pT', pk)],
                                         bias=bias4[gk][:, i - i0, q:q + 1], scale=SCALE)
                                if ip == i:
                                    self.tt('pool', pT[pk][:], pT[pk][:], self.mfox[:, r * 128:(r + 1) * 128], ALU.mult,
                                            [('pT', pk), 'mfox'], [('pT', pk)])
                                ob = 2 + (i - i0)
                                self.mm(self.ps[ob][:, 0:129], pT[pk][:], VVk[:, r, ip * 129:ip * 129 + 129],
                                        (ip == 0 and r == 0), (ip == i and r == 3), [('pT', pk)] + VVr, [('ps', ob)])
                                if ip == i and r == 3:
                                    rc = rden[:, (i - i0):(i - i0) + 1]
                                    self.S.op('dve', (lambda a, b: (lambda e: e.reciprocal(out=a, in_=b)))(rc, self.ps[ob][:, 128:129]),
                                              [('ps', ob)], [('rden', i - i0)])
                                    self.ts('dve', merged[:, i, h * 128:(h + 1) * 128], self.ps[ob][:, 0:128], rc, None, ALU.mult, None,
                                            [('ps', ob), ('rden', i - i0)], [('merged', i, h)])
                else:
                    steps = [(i, n) for i in range(NB) for n in range(i + 1)]
                    NS = len(steps)

                    def st0(m):
                        i, n = steps[m]
                        ip = i - n
                        bank = m % 2
                        ak = m % NA
                        self.mm(self.ps[bank][:, 0:512].rearrange("p (a b) -> p a b", a=4), QTk[:, i * 128:(i + 1) * 128],
                                KTk[:, :, ip * 128:(ip + 1) * 128], True, True, KTr + [('QT', k)], [('ps', bank)])
                        self.act(a_[ak][:], self.ps[bank][:, 0:512], AF.Sigmoid, [('ps', bank)], [('a', ak)], scale=-SCALE)
                        if n == 0:
                            self.tt('dve', a_[ak][:], a_[ak][:], self.nmsb, ALU.max, [('a', ak), 'cst_f2'], [('a', ak)])

                    def st1(m):
                        i, n = steps[m]
                        ak = m % NA
                        pk_ = (m - 1) % NA
                        if n == 0:
                            ini = 1.0
                            rd = [('a', ak), 'zeros']
                            self.memset('pool', buf[ak][:, 0:1], 1.0, [('bufc', ak)])
                        else:
                            ini = buf[pk_][:, 512:513]
                            rd = [('a', ak), 'zeros', ('buf', pk_)]
                            self.cp('pool', buf[ak][:, 0:1], buf[pk_][:, 512:513], [('buf', pk_)], [('bufc', ak)])
                        self.S.op('dve', (lambda o, d0, ini_: (lambda e: e.tensor_tensor_scan(out=o, data0=d0, data1=self.zeros[:], initial=ini_,
                                                                                            op0=ALU.mult, op1=ALU.add)))(
                            buf[ak][:, 1:513], a_[ak][:], ini), rd, [('buf', ak)])

                    def st2(m):
                        ak = m % NA
                        bank = 2 + m % 2
                        for s in range(4):
                            self.mm(self.ps[bank][:, s * 128:(s + 1) * 128], buf[ak][:, s * 128:s * 128 + 128], self.identb[:], True, False,
                                    [('buf', ak), ('bufc', ak), 'identb'], [('ps', bank)])
                            self.mm(self.ps[bank][:, s * 128:(s + 1) * 128], buf[ak][:, s * 128 + 1:s * 128 + 129], self.nidentb[:], False, True,
                                    [('buf', ak), 'nidentb'], [('ps', bank)])
                        self.cp('act', AT[m % NAB][:], self.ps[bank][:, 0:512], [('ps', bank)], [('AT', m % NAB)])

                    def st3(m):
                        i, n = steps[m]
                        ip = i - n
                        ob = 4 + i % 2
                        for s in range(4):
                            self.mm(self.ps[ob][:, 0:128], AT[m % NAB][:, s * 128:(s + 1) * 128], VVk[:, s, ip * 128:(ip + 1) * 128],
                                    (n == 0 and s == 0), (n == i and s == 3), [('AT', m % NAB)] + VVr, [('ps', ob)])
                        if n == i:
                            self.cp('dve', merged[:, i, 1024 + h * 128:1024 + (h + 1) * 128], self.ps[ob][:, 0:128], [('ps', ob)],
                                    [('merged', i, 8 + h)])

                    for m in range(NS + 3):
                        if m < NS:
                            st0(m)
                        if 0 <= m - 1 < NS:
                            st1(m - 1)
                        if 0 <= m - 2 < NS:
                            st2(m - 2)
                        if 0 <= m - 3 < NS:
                            st3(m - 3)
            if 'merged' in self.dbg_out and l == 0:
                self.S.drain()
                self.dma(self.dbg_out['merged'], merged[:], [], ['dbgm'])
            self.flush()

    def phase_C2(self, l, mT, xsrc, mid=None):
        nc = self.nc
        NB, NT = self.NB, self.NT
        W = self.I['w_out'][l]
        with ExitStack() as es:
            sb = lambda n, s, dt: es.enter_context(nc.sbuf_tensor(f"C2{l}_{n}", s, dt))
            wst = [sb(f"wst{k}", [128, KC, 256], F32) for k in range(2)]
            wb = [sb(f"wb{k}", [128, KC, 256], BF16) for k in range(2)]
            xin = [sb(f"xin{k}", [128, 256], F32) for k in range(3)]
            xo = [sb(f"xo{k}", [128, 256], F32) for k in range(3)]
            self.wload(W, KC, 0, 256, wst[0], wb[0], 0, 'w')
            if mid is not None:
                mid(es)
            n = 0
            for cg in range(8):
                slot = cg % 2
                if cg + 1 < 8:
                    self.wload(W, KC, (cg + 1) * 256, 256, wst[(cg + 1) % 2], wb[(cg + 1) % 2], (cg + 1) % 2, 'w')
                for lb in range(NB):
                    bank = n % 4
                    k3 = n % 3
                    n += 1
                    self.dma(xin[k3][:], xsrc[lb * 128:(lb + 1) * 128, cg * 256:(cg + 1) * 256], [], [('xin', k3)])
                    for kc in range(KC):
                        self.mm(self.ps[bank][:, 0:256], mT[:, kc, lb * 128:(lb + 1) * 128], wb[slot][:, kc, :], kc == 0, kc == KC - 1,
                                [('w', slot), ('dstT', lb, kc // 8)], [('ps', bank)])
                    self.tt('dve', xo[k3][:], self.ps[bank][:, 0:256], xin[k3][:], ALU.add, [('ps', bank), ('xin', k3)], [('xo', k3)])
                    self.dma(self.xa_d[lb * 128:(lb + 1) * 128, cg * 256:(cg + 1) * 256], xo[k3][:], [('xo', k3)], [('xa_d', lb, cg)])
            if 'xa' in self.dbg_out and l == 0:
                self.S.drain()
                self.dma(self.dbg_out['xa'], self.xa_d, [], ['dbgxa'])
            self.flush()

    def phase_halo(self, l, halo_s):
        nc = self.nc
        NB = self.NB
        par = l % 2
        rows = 4 * NB * 2
        with ExitStack() as es:
            Hall = es.enter_context(nc.sbuf_tensor(f"Hall{l}", [128, D], BF16))
            self.dma(Hall[0:rows, :], self.gout3[par], [], ['Hall'])
            for kc in range(KC):
                bank = kc % 2
                self.mm(self.ps[bank][:, 0:2 * NB], Hall[0:rows, kc * 128:(kc + 1) * 128], self.selb[0:rows, :], True, True,
                        ['Hall', 'selb'], [('ps', bank)])
                self.cp('dve', halo_s[:, kc].rearrange("p a b -> p (a b)"), self.ps[bank][:, 0:2 * NB], [('ps', bank)], [('halo', kc)])
            self.flush()

    def phase_FFN(self, l, halo_s, xdst):
        nc = self.nc
        NB, NT = self.NB, self.NT
        I = self.I
        Wu = I['w_up'][l]
        Wd = I['w_down'][l]
        TB = min(8, NB)
        ntile = NB // TB
        cts = []
        b = 0
        while b < TB:
            nb_ = min(3, TB - b)
            cts.append((b, nb_))
            b += nb_
        with ExitStack() as es:
            sb = lambda n, s, dt: es.enter_context(nc.sbuf_tensor(f"F{l}_{n}", s, dt))
            gT = sb("gT", [128, FC, TB * 128], BF16)
            cwT = sb("cwT", [128, 2 * FC, 4], F32)
            with ExitStack() as es0:
                QC = 22
                crow = [es0.enter_context(nc.sbuf_tensor(f"F{l}_crow{k}", [4, QC * 128], F32)) for k in range(2)]
                for qd in range(2 * FC // QC):
                    k2 = qd % 2
                    bank = 6 + k2
                    self.dma(crow[k2][:], I['conv_wb'][l][:, qd * QC * 128:(qd + 1) * QC * 128], [], [('crow', k2)])
                    for c in range(QC):
                        self.mm(self.ps[bank][:, c * 4:(c + 1) * 4], crow[k2][0:4, c * 128:(c + 1) * 128], self.ident_f[0:4, 0:4], True, True,
                                [('crow', k2), 'cst_f'], [('ps', bank)], tr=True)
                    self.cp('dve', cwT[:, qd * QC:(qd + 1) * QC, :], self.ps[bank][:, 0:QC * 4].rearrange("p (a b) -> p a b", b=4),
                            [('ps', bank)], [('cwT', qd)])
                self.flush()
            for T in range(ntile):
                lb0 = T * TB
                with ExitStack() as es1:
                    sb1 = lambda n, s, dt: es1.enter_context(nc.sbuf_tensor(f"F{l}_{T}_{n}", s, dt))
                    h2T = sb1("h2T", [128, KC, TB, 130], BF16)
                    wstg = sb1("wstg", [128, KC, 256], F32)
                    wstv = sb1("wstv", [128, KC, 256], F32)
                    wbg = [sb1(f"wbg{k}", [128, KC, 256], BF16) for k in range(2)]
                    wbv = [sb1(f"wbv{k}", [128, KC, 256], BF16) for k in range(2)]
                    tg = [sb1(f"tg{k}", [128, 3, 128], F32) for k in range(3)]
                    tv = [sb1(f"tv{k}", [128, 3, 128], F32) for k in range(3)]

                    def up_load(ii, slot):
                        srcg = Wu[:, ii * 256:(ii + 1) * 256].rearrange("(kc p) c -> p kc c", p=128)
                        srcv = Wu[:, DFF + ii * 256:DFF + (ii + 1) * 256].rearrange("(kc p) c -> p kc c", p=128)
                        self.dma(wstg[:], srcg, [], ['wstg'])
                        self.dma(wstv[:], srcv, [], ['wstv'])
                        self.cp('act', wbg[slot][:], wstg[:], ['wstg'], [('wbg', slot)])
                        self.cp('dve', wbv[slot][:], wstv[:], ['wstv'], [('wbv', slot)])

                    self.dma(h2T[:], self.h2T_d[:, :, lb0:lb0 + TB, :], [], ['h2T'])
                    for kc in range(KC):
                        self.cp('pool', h2T[:, kc, :, 0:2], halo_s[:, kc, lb0:lb0 + TB, :], ['h2T'], ['h2T'])
                    up_load(0, 0)
                    epn = 0
                    for i in range(FC):
                        slot = (i // 2) % 2
                        hh_ = i % 2
                        if i % 2 == 0 and i // 2 + 1 < FC // 2:
                            up_load(i // 2 + 1, (i // 2 + 1) % 2)
                        for cti, (b0, nbk) in enumerate(cts):
                            ncol = nbk * 130
                            bg = (2 * epn) % 6
                            bv = (2 * epn + 1) % 6
                            e2 = epn % 3
                            epn += 1
                            for kc in range(KC):
                                self.mm(self.ps[bg][:, 0:ncol].rearrange("p (a b) -> p a b", b=130), wbg[slot][:, kc, hh_ * 128:(hh_ + 1) * 128], h2T[:, kc, b0:b0 + nbk, :],
                                        kc == 0, kc == KC - 1, [('wbg', slot), 'h2T'], [('ps', bg)])
                            for kc in range(KC):
                                self.mm(self.ps[bv][:, 0:ncol].rearrange("p (a b) -> p a b", b=130), wbv[slot][:, kc, hh_ * 128:(hh_ + 1) * 128], h2T[:, kc, b0:b0 + nbk, :],
                                        kc == 0, kc == KC - 1, [('wbv', slot), 'h2T'], [('ps', bv)])
                            for (bank, t_, ch, trn) in ((bg, tg[e2], i, ('tg', e2)), (bv, tv[e2], FC + i, ('tv', e2))):
                                u = self.ps[bank][:, 0:ncol].rearrange("p (a b) -> p a b", b=130)
                                to = t_[:, 0:nbk, :]
                                self.act(to, u[:, :, 2:130], AF.Identity, [('ps', bank)], [trn], bias=cwT[:, ch, 3:4], scale=cwT[:, ch, 2:3])
                                self.stt(to, u[:, :, 1:129], cwT[:, ch, 1:2], to, ALU.mult, ALU.add, [('ps', bank), trn], [trn])
                                self.stt(to, u[:, :, 0:128], cwT[:, ch, 0:1], to, ALU.mult, ALU.add, [('ps', bank), trn], [trn])
                            self.act(tg[e2][:, 0:nbk, :], tg[e2][:, 0:nbk, :], AF.Silu, [('tg', e2)], [('tg', e2)])
                            self.tt('pool', gT[:, i, b0 * 128:(b0 + nbk) * 128].rearrange("p (a b) -> p a b", b=128), tg[e2][:, 0:nbk, :], tv[e2][:, 0:nbk, :],
                                    ALU.mult, [('tg', e2), ('tv', e2)], [('gT', i)])
                    self.flush()
                self.down_proj(l, T, TB, gT, Wd, xdst)

    def down_proj(self, l, T, TB, gT, Wd, xdst):
        nc = self.nc
        lb0 = T * TB
        with ExitStack() as es:
            sb = lambda n, s, dt: es.enter_context(nc.sbuf_tensor(f"Dn{l}_{T}_{n}", s, dt))
            wst = [sb(f"wst{k}", [128, 11, 256], F32) for k in range(2)]
            wd = [sb(f"wd{k}", [128, FC, 256], BF16) for k in range(2)]
            xin = [sb(f"xin{k}", [128, 256], F32) for k in range(3)]
            xo = [sb(f"xo{k}", [128, 256], F32) for k in range(3)]
            pn = [0]

            def dload(cg, slot):
                for pc in range(4):
                    s2 = pn[0] % 2
                    pn[0] += 1
                    src = Wd[pc * 11 * 128:(pc + 1) * 11 * 128, cg * 256:(cg + 1) * 256].rearrange("(kc p) c -> p kc c", p=128)
                    self.dma(wst[s2][:], src, [], [('dwst', s2)])
                    self.cp(('act', 'dve', 'act', 'pool')[pc], wd[slot][:, pc * 11:(pc + 1) * 11, :], wst[s2][:], [('dwst', s2)], [('wd', slot, pc)])
            dload(0, 0)
            n = 0
            for cg in range(8):
                slot = cg % 2
                if cg + 1 < 8:
                    dload(cg + 1, (cg + 1) % 2)
                wr = [('wd', slot, pc) for pc in range(4)]
                for b in range(TB):
                    lb = lb0 + b
                    bank = 4 + n % 4
                    k3 = n % 3
                    n += 1
                    self.dma(xin[k3][:], self.xa_d[lb * 128:(lb + 1) * 128, cg * 256:(cg + 1) * 256], [], [('xin', k3)])
                    for i in range(FC):
                        self.mm(self.ps[bank][:, 0:256], gT[:, i, b * 128:(b + 1) * 128], wd[slot][:, i, :], i == 0, i == FC - 1,
                                wr + [('gT', i)], [('ps', bank)])
                    self.tt('dve', xo[k3][:], self.ps[bank][:, 0:256], xin[k3][:], ALU.add, [('ps', bank), ('xin', k3)], [('xo', k3)])
                    self.dma(xdst[lb * 128:(lb + 1) * 128, cg * 256:(cg + 1) * 256], xo[k3][:], [('xo', k3)], [('xdst', lb, cg)])
            self.flush()


def core_consts(j, NB):
    NQ = 4 * NB
    cst = np.zeros((128, 1408), np.float32)
    cst[:, 0:128] = np.eye(128, dtype=np.float32)
    cst[:, 128:256] = np.eye(128, dtype=np.float32)[::-1]
    s = np.arange(128)
    cst[:, 256:384] = (s[:, None] <= s[None, :]).astype(np.float32)
    for r in range(4):
        if r < j:
            m = np.ones((128, 128), np.float32)
        elif r == j:
            m = (s[:, None] <= s[None, :]).astype(np.float32)
        else:
            m = np.zeros((128, 128), np.float32)
        cst[:, 384 + r * 128:384 + (r + 1) * 128] = m
    for slot in range(4):
        r = 3 - slot
        if r > j:
            m = np.ones((128, 128), np.float32)
        elif r == j:
            sp = 127 - s
            m = (sp[None, :] >= s[:, None]).astype(np.float32)
        else:
            m = np.zeros((128, 128), np.float32)
        cst[:, 896 + slot * 128:896 + (slot + 1) * 128] = m
    cst2 = np.zeros((128, 256), np.float32)
    q = np.arange(NQ)
    gq = 4 * (q % NB) + q // NB
    if NQ <= 64:
        cst2[0:NQ, 0:NQ] = (gq[:, None] < gq[None, :]).astype(np.float32)
    own_g = 4 * np.arange(NB) + j
    cst2[0:NQ, 64:64 + NB] = (gq[:, None] < own_g[None, :]).astype(np.float32)
    for r in range(4):
        cst2[:, 200 + r] = -30000.0 if r > j else 0.0
    for lb in range(NB):
        for k in range(2):
            if j >= 1:
                r_, lb_ = j - 1, lb
            else:
                r_, lb_ = 3, lb - 1
            if lb_ < 0:
                continue
            cst2[r_ * NB * 2 + lb_ * 2 + k, 128 + lb * 2 + k] = 1.0
    return cst, cst2


def make_in_maps(inputs, L, NB, ncores=8):
    x = np.asarray(inputs['x'], np.float32)
    f = lambda k: np.ascontiguousarray(np.asarray(inputs[k], np.float32)[:L])
    shared = {
        'attn_norm': f('attn_norm'), 'w_in': f('w_in'), 'b_forget': f('b_forget'), 'q_norm': f('q_norm'),
        'k_norm': f('k_norm'),
        'out_norm': np.ascontiguousarray(np.concatenate([f('out_norm_fox'), f('out_norm_sb')], axis=1)),
        'w_out': f('w_out'), 'ffn_norm': f('ffn_norm'), 'w_up': f('w_up'),
        'conv_wb': np.ascontiguousarray(np.concatenate([f('conv_w'), f('conv_b')[:, None, :]], axis=1)),
        'w_down': f('w_down'),
    }
    maps = []
    for c in range(ncores):
        b, j = c // 4, c % 4
        rows = np.concatenate([np.arange((4 * lb + j) * 128, (4 * lb + j + 1) * 128) for lb in range(NB)])
        cst, cst2 = core_consts(j, NB)
        m = dict(shared)
        m['x'] = np.ascontiguousarray(x[b][rows])
        m['cst'] = cst
        m['cst2'] = cst2
        maps.append(m)
    return maps


def gather_out(results, B, S, NB, ncores=8):
    out = np.zeros((B, S, D), np.float32)
    for c in range(ncores):
        b, j = c // 4, c % 4
        yc = np.asarray(results[c]['y'])
        for lb in range(NB):
            g = 4 * lb + j
            out[b, g * 128:(g + 1) * 128] = yc[lb * 128:(lb + 1) * 128]
    return out


_CACHE = {}


def kernel(**inputs):
    L = int(np.asarray(inputs['w_in']).shape[0])
    Bn, S, _ = np.asarray(inputs['x']).shape
    NB = S // 512
    key = (L, NB)
    if key not in _CACHE:
        _CACHE[key] = Builder(L, NB).build()
    nc = _CACHE[key]
    in_maps = make_in_maps(inputs, L, NB)
    res = run_bass_kernel_spmd(nc, in_maps, core_ids=list(range(8)))
    return gather_out(res.results, Bn, S, NB)
```
